# Optimizing a Trainium2 kernel written in Bass

```python
import jax, jax.numpy as jnp
from jax import lax
import numpy as np

D_MODEL = 4096
BATCH = 1
SEQ = 8192
DEPTH = 2
DEC_BATCH = 1
DEC_SEQ = 16384
PAST_LEN = 128

D_POOL = D_MODEL // 4
D_MLSTM = D_MODEL // 4
D_MLA = D_MODEL // 2
POOL_WINDOWS = (2, 4, 8, 16)
POOL_GROUPS = 4
POOL_GW = D_POOL // POOL_GROUPS
MLSTM_HEADS = 4
MLSTM_HD = D_MLSTM // MLSTM_HEADS
MLSTM_CHUNK = 128
MLSTM_GATES = 4 * MLSTM_HEADS
MLA_HEADS = 16
MLA_V = D_MLA // MLA_HEADS
MLA_NOPE = 128
MLA_ROPE = 64
MLA_QK = MLA_NOPE + MLA_ROPE
Q_LORA = 1536
KV_LORA = 512
ROPE_THETA = 10000.0
Q_BLOCK = 128
NORM_EPS = 1e-6
IN_SIZES = (D_POOL, D_POOL,
            D_MLSTM, D_MLSTM, D_MLSTM, D_MLSTM, D_MLSTM,
            MLSTM_GATES,
            Q_LORA, KV_LORA, MLA_ROPE, D_MLA)
N_IN = 2 * D_POOL + 5 * D_MLSTM + MLSTM_GATES + Q_LORA + KV_LORA + MLA_ROPE + D_MLA

kernel_name = "hybrid_pool_mlstm_mla_encoder"


def _rms_norm(x, g):
    xf = x.astype(jnp.float32)
    y = xf * lax.rsqrt(jnp.mean(xf * xf, axis=-1, keepdims=True) + NORM_EPS)
    return (y * g.astype(jnp.float32)).astype(x.dtype)


def _rope_tables(S):
    pos = jnp.arange(S, dtype=jnp.float32)
    inv_freq = ROPE_THETA ** (-(jnp.arange(0, MLA_ROPE, 2, dtype=jnp.float32) / MLA_ROPE))
    ang = pos[:, None] * inv_freq[None, :]
    return jnp.cos(ang), jnp.sin(ang)


def _rope(x, cos, sin):
    xf = x.astype(jnp.float32)
    half = xf.shape[-1] // 2
    x1, x2 = xf[..., :half], xf[..., half:]
    return jnp.concatenate([x1 * cos - x2 * sin, x2 * cos + x1 * sin], axis=-1).astype(x.dtype)


def _pool_mixer(xp, pool_w, pool_scale):
    B, S, _ = xp.shape
    xf = xp.astype(jnp.float32)
    cs = jnp.concatenate([jnp.zeros((B, 1, D_POOL), jnp.float32), jnp.cumsum(xf, axis=1)], axis=1)
    pos = jnp.arange(S)
    outs = []
    for g, w in enumerate(POOL_WINDOWS):
        lo = jnp.clip(pos - w // 2, 0, S)
        hi = jnp.clip(pos + w // 2, 0, S)
        sl = slice(g * POOL_GW, (g + 1) * POOL_GW)
        csg = cs[..., sl]
        mean = (csg[:, hi] - csg[:, lo]) / (hi - lo).astype(jnp.float32)[None, :, None]
        diff = (mean - xf[..., sl]).astype(xp.dtype)
        outs.append(jnp.einsum('bsc,cd->bsd', diff, pool_w[g]))
    return (jnp.concatenate(outs, axis=-1) * pool_scale).astype(xp.dtype)


def _mlstm_chunk_scan(q, k, v, i_pre, logf):
    B, H, S, d = q.shape
    L = MLSTM_CHUNK
    nc = S // L

    def to_chunks(a):
        return jnp.moveaxis(a.reshape(B, H, nc, L, *a.shape[3:]), 2, 0)

    xs = (to_chunks(q), to_chunks(k), to_chunks(v), to_chunks(i_pre), to_chunks(logf))
    causal = jnp.tril(jnp.ones((L, L), dtype=bool))

    def step(carry, inp):
        C, n, m = carry
        qc, kc, vc, ic, fc = inp
        b = jnp.cumsum(fc, axis=-1)
        D = jnp.where(causal, b[..., :, None] - b[..., None, :] + ic[..., None, :], -jnp.inf)
        inter = b + m[..., None]
        m_t = jnp.maximum(inter, jnp.max(D, axis=-1))
        w = jnp.exp(D - m_t[..., None]) * jnp.einsum('bhtd,bhsd->bhts', qc, kc)
        ei = jnp.exp(inter - m_t)
        num = ei[..., None] * jnp.einsum('bhtd,bhde->bhte', qc, C) + jnp.einsum('bhts,bhse->bhte', w, vc)
        den = ei * jnp.einsum('bhtd,bhd->bht', qc, n) + jnp.sum(w, axis=-1)
        h = num / jnp.maximum(jnp.abs(den), jnp.exp(-m_t))[..., None]
        bL = b[..., -1]
        a = bL[..., None] - b + ic
        m_new = jnp.maximum(bL + m, jnp.max(a, axis=-1))
        decay = jnp.exp(bL + m - m_new)
        ea = jnp.exp(a - m_new[..., None])
        C = decay[..., None, None] * C + jnp.einsum('bhs,bhsd,bhse->bhde', ea, kc, vc)
        n = decay[..., None] * n + jnp.einsum('bhs,bhsd->bhd', ea, kc)
        return (C, n, m_new), h

    init = (jnp.zeros((B, H, d, d), jnp.float32), jnp.zeros((B, H, d), jnp.float32),
            jnp.zeros((B, H), jnp.float32))
    _, hs = lax.scan(step, init, xs)
    return jnp.moveaxis(hs, 0, 2).reshape(B, H, S, d)


def _mlstm_mixer(q, k, v, o, gates, gate_bias, norm_g):
    B, S, _ = q.shape

    def heads(a):
        return a.astype(jnp.float32).reshape(B, S, MLSTM_HEADS, MLSTM_HD).transpose(0, 2, 1, 3)

    qh, kh, vh = heads(q), heads(k) * (MLSTM_HD ** -0.5), heads(v)
    g = (gates.astype(jnp.float32) + gate_bias.astype(jnp.float32)).reshape(B, S, 4, MLSTM_HEADS)
    g = g.transpose(2, 0, 3, 1)
    h_fwd = _mlstm_chunk_scan(qh, kh, vh, g[0], jax.nn.log_sigmoid(g[1]))
    flip = lambda a: jnp.flip(a, axis=2)
    h_bwd = flip(_mlstm_chunk_scan(flip(qh), flip(kh), flip(vh), flip(g[2]), flip(jax.nn.log_sigmoid(g[3]))))
    h = (h_fwd + h_bwd).transpose(0, 2, 1, 3)
    h = _rms_norm(h, norm_g.reshape(MLSTM_HEADS, MLSTM_HD)).reshape(B, S, D_MLSTM)
    return (jax.nn.sigmoid(o.astype(jnp.float32)) * h).astype(q.dtype)


def _mla_mixer(q_lat, kv_lat, k_rope, qlat_g, w_uq, kvlat_g, w_ukv, qn_g, qr_g, kn_g, kr_g, cos, sin):
    B, S, _ = q_lat.shape
    q = (_rms_norm(q_lat, qlat_g) @ w_uq).reshape(B, S, MLA_HEADS, MLA_QK)
    kv = (_rms_norm(kv_lat, kvlat_g) @ w_ukv).reshape(B, S, MLA_HEADS, MLA_NOPE + MLA_V)
    cq, sq = cos[None, :, None, :], sin[None, :, None, :]
    q_nope = _rms_norm(q[..., :MLA_NOPE], qn_g)
    q_pe = _rope(_rms_norm(q[..., MLA_NOPE:], qr_g), cq, sq)
    k_nope = _rms_norm(kv[..., :MLA_NOPE], kn_g)
    v = kv[..., MLA_NOPE:]
    k_pe = _rope(_rms_norm(k_rope, kr_g), cos[None], sin[None])
    k_pe = jnp.broadcast_to(k_pe[:, :, None, :], (B, S, MLA_HEADS, MLA_ROPE))
    qf = jnp.concatenate([q_nope, q_pe], axis=-1)
    kt = jnp.concatenate([k_nope, k_pe], axis=-1).transpose(0, 2, 1, 3)
    vt = v.transpose(0, 2, 1, 3)
    nb = S // Q_BLOCK
    qb = qf.reshape(B, nb, Q_BLOCK, MLA_HEADS, MLA_QK).transpose(1, 0, 3, 2, 4)
    scale = MLA_QK ** -0.5

    def attend(qblk):
        s = jnp.einsum('bhqd,bhkd->bhqk', qblk, kt, preferred_element_type=jnp.float32) * scale
        p = jax.nn.softmax(s, axis=-1)
        return jnp.einsum('bhqk,bhkd->bhqd', p.astype(vt.dtype), vt)

    out = lax.map(attend, qb)
    return out.transpose(1, 0, 3, 2, 4).reshape(B, S, D_MLA)


def _layer(x, cos, sin, norm_g, w_in, gate_bias, pool_w, pool_scale, mlstm_norm_g,
           qlat_g, w_uq, kvlat_g, w_ukv, qn_g, qr_g, kn_g, kr_g, w_out):
    h = _rms_norm(x, norm_g)
    u = h @ w_in
    idx = np.cumsum(IN_SIZES)[:-1].tolist()
    (p_x, p_z, m_q, m_k, m_v, m_o, m_z, m_g, a_qlat, a_kvlat, a_krope, a_z) = jnp.split(u, idx, axis=-1)
    pool_out = _pool_mixer(p_x, pool_w, pool_scale) * jax.nn.silu(p_z)
    mlstm_out = _mlstm_mixer(m_q, m_k, m_v, m_o, m_g, gate_bias, mlstm_norm_g) * jax.nn.silu(m_z)
    mla_out = _mla_mixer(a_qlat, a_kvlat, a_krope, qlat_g, w_uq, kvlat_g, w_ukv,
                         qn_g, qr_g, kn_g, kr_g, cos, sin) * jax.nn.silu(a_z)
    mix = jnp.concatenate([pool_out.astype(x.dtype), mlstm_out.astype(x.dtype), mla_out.astype(x.dtype)], axis=-1)
    return x + mix @ w_out


def _trunk(x, params):
    cos, sin = _rope_tables(x.shape[1])
    for l in range(DEPTH):
        x = _layer(x, cos, sin, *[p[l] for p in params])
    return x


def setup_inputs(seed: int = 0) -> dict:
    key = jax.random.key(seed)
    ks = jax.random.split(key, 20)
    f32 = jnp.float32
    nrm = lambda k, shape, s: s * jax.random.normal(k, shape, f32)
    gain = lambda k, shape: 1.0 + 0.02 * jax.random.normal(k, shape, f32)
    gate_base = jnp.array([0.0, 3.0, 0.0, 3.0], f32)[None, :, None]
    gate_bias = (gate_base + nrm(ks[4], (DEPTH, 4, MLSTM_HEADS), 0.5)).reshape(DEPTH, MLSTM_GATES)
    return {
        "x_prompt": nrm(ks[0], (BATCH, SEQ, D_MODEL), 1.0),
        "x_sample": nrm(ks[1], (DEC_BATCH, DEC_SEQ, D_MODEL), 1.0),
        "norm_g": gain(ks[2], (DEPTH, D_MODEL)),
        "w_in": nrm(ks[3], (DEPTH, D_MODEL, N_IN), D_MODEL ** -0.5),
        "gate_bias": gate_bias,
        "pool_w": nrm(ks[5], (DEPTH, POOL_GROUPS, POOL_GW, POOL_GW), POOL_GW ** -0.5),
        "pool_scale": gain(ks[6], (DEPTH, D_POOL)),
        "mlstm_norm_g": gain(ks[7], (DEPTH, D_MLSTM)),
        "qlat_g": gain(ks[8], (DEPTH, Q_LORA)),
        "w_uq": nrm(ks[9], (DEPTH, Q_LORA, MLA_HEADS * MLA_QK), Q_LORA ** -0.5),
        "kvlat_g": gain(ks[10], (DEPTH, KV_LORA)),
        "w_ukv": nrm(ks[11], (DEPTH, KV_LORA, MLA_HEADS * (MLA_NOPE + MLA_V)), KV_LORA ** -0.5),
        "qn_g": gain(ks[12], (DEPTH, MLA_NOPE)),
        "qr_g": gain(ks[13], (DEPTH, MLA_ROPE)),
        "kn_g": gain(ks[14], (DEPTH, MLA_NOPE)),
        "kr_g": gain(ks[15], (DEPTH, MLA_ROPE)),
        "w_out": nrm(ks[16], (DEPTH, D_MODEL, D_MODEL), D_MODEL ** -0.5),
    }


def reference(x_prompt, x_sample, norm_g, w_in, gate_bias, pool_w, pool_scale, mlstm_norm_g,
              qlat_g, w_uq, kvlat_g, w_ukv, qn_g, qr_g, kn_g, kr_g, w_out):
    params = (norm_g, w_in, gate_bias, pool_w, pool_scale, mlstm_norm_g,
              qlat_g, w_uq, kvlat_g, w_ukv, qn_g, qr_g, kn_g, kr_g, w_out)
    y_prompt = _trunk(x_prompt, params)
    y_sample = _trunk(x_sample, params)
    return (y_prompt, y_sample)
```

```python
import math
from contextlib import ExitStack
import numpy as np
import ml_dtypes
import concourse.bass as bass
import concourse.mybir as mybir
from concourse.bass_utils import run_bass_kernel_spmd

F32 = mybir.dt.float32
BF16 = mybir.dt.bfloat16
AF = mybir.ActivationFunctionType
ALU = mybir.AluOpType

NCORES = 8
D = 4096
KC = 32
DEPTH = 2
NIN = 11344
EPS = 1e-6
NPC = 72


class Buf:
    __slots__ = ("name", "writers", "readers", "dsem")

    def __init__(self, name):
        self.name = name
        self.writers = {}
        self.readers = {}
        self.dsem = None


class FW:
    def __init__(self, nc, stack):
        self.nc = nc
        self.stack = stack
        self.eng = {"pe": nc.tensor, "act": nc.scalar, "dve": nc.vector, "pool": nc.gpsimd, "sp": nc.sync}
        self.ecount = {k: 0 for k in self.eng}
        self.sems = {}
        for k in self.eng:
            self.sems[("e", k)] = stack.enter_context(nc.semaphore("es_" + k))
        self.issued = {}
        self.waited = {k: {} for k in self.eng}
        self.ninst = 0
        self.nbuf = 0

    def new_dsem(self, name):
        key = ("d", name)
        self.sems[key] = self.stack.enter_context(self.nc.semaphore("ds_" + name))
        self.issued[key] = 0
        return key

    def buf(self, name, dma=False):
        self.nbuf += 1
        b = Buf(name)
        if dma:
            if not hasattr(self, "pool_keys"):
                self.pool_keys = [self.new_dsem("p%d" % i) for i in range(40)]
                self.pool_i = 0
            b.dsem = self.pool_keys[self.pool_i % len(self.pool_keys)]
            self.pool_i += 1
        return b

    def _wait(self, ek, toks):
        need = {}
        for (key, val) in toks:
            if key[0] == "d":
                val = self.issued[key]
            if key == ("e", "pe") and ek == "pe":
                continue
            if val > need.get(key, 0):
                need[key] = val
        w = self.waited[ek]
        for key, val in need.items():
            if w.get(key, 0) >= val:
                continue
            self.eng[ek].wait_ge(self.sems[key], val)
            w[key] = val

    def _deps(self, reads, writes):
        toks = []
        for b in reads:
            toks += list(b.writers.items())
        for b in writes:
            toks += list(b.writers.items())
            toks += list(b.readers.items())
        return toks

    def _commit(self, tok, reads, writes):
        k, v = tok
        for b in reads:
            if b.readers.get(k, 0) < v:
                b.readers[k] = v
        for b in writes:
            b.writers = {k: v}
            b.readers = {}

    def op(self, ek, fn, reads=(), writes=()):
        self._wait(ek, self._deps(reads, writes))
        ins = fn(self.eng[ek])
        self.ecount[ek] += 1
        ins.then_inc(self.sems[("e", ek)], 1)
        tok = (("e", ek), self.ecount[ek])
        self._commit(tok, reads, writes)
        self.ninst += 1
        return tok

    def mm(self, fns, reads=(), writes=()):
        self._wait("pe", self._deps(reads, writes))
        pe = self.eng["pe"]
        ins = None
        for fn in fns:
            ins = fn(pe)
        self.ecount["pe"] += 1
        ins.then_inc(self.sems[("e", "pe")], 1)
        tok = (("e", "pe"), self.ecount["pe"])
        self._commit(tok, reads, writes)
        self.ninst += len(fns)
        return tok

    def dma(self, qk, out_ap, in_ap, side, reads=(), writes=(), **kw):
        self._wait(qk, self._deps(reads, writes))
        key = side.dsem
        ins = self.eng[qk].dma_start(out=out_ap, in_=in_ap, **kw)
        ins.then_inc(self.sems[key], 16)
        self.issued[key] += 16
        tok = (key, self.issued[key])
        self._commit(tok, reads, writes)
        self.ninst += 1
        return tok

    def collective(self, key, fn):
        self.drain("pool")
        ins = fn(self.eng["pool"])
        ins.then_inc(self.sems[key], 1)
        self.issued[key] += 1

    def drain(self, ek, skip=()):
        toks = [(("e", k), self.ecount[k]) for k in self.eng if self.ecount[k] > 0 and k != ek]
        toks += [(k, v) for k, v in self.issued.items() if v > 0 and k not in skip]
        self._wait(ek, toks)

    def barrier(self, skip=()):
        for ek in self.eng:
            self.drain(ek, skip)
        self.pool_i = 0


def subtiles_of(np_, ns):
    out = []
    for (s0, n) in ((0, np_), (np_, ns)):
        t = 0
        while t < n:
            w = min(512, n - t)
            out.append((s0 + t, w))
            t += w
    return out


def build(np_, ns, dbg=()):
    NT = np_ + ns
    NB = NT // 128
    subs = subtiles_of(np_, ns)
    nc = bass.Bass("TRN2", target_bir_lowering=False)
    dt_in = lambda name, shape, dt=F32: nc.dram_tensor(name, shape, dt, kind="ExternalInput").ap()
    x_in = dt_in("x", [NT, D])
    w_in = dt_in("w_in", [DEPTH, D, NIN])
    w_out = dt_in("w_out", [DEPTH, D, D])
    w_uq = dt_in("w_uq", [DEPTH, 1536, 3072])
    w_ukv = dt_in("w_ukv", [DEPTH, 512, 4096])
    pool_w = dt_in("pool_w", [DEPTH, 4, 256, 256])
    pcols = dt_in("pcols", [DEPTH, 128, NPC])
    cosT = dt_in("cosT", [64, NT])
    sinT = dt_in("sinT", [64, NT])
    ident_in = dt_in("ident", [128, 128], BF16)
    sel_in = dt_in("sel", [128, 16])
    pcorr_in = dt_in("pcorr", [128, 128])
    selm_in = dt_in("selm", [128, 16])
    mask_in = dt_in("dmask", [128, 2 * 4 * 512], BF16)
    y_out = nc.dram_tensor("y", [NT, D], F32, kind="ExternalOutput").ap()

    scr = lambda name, shape, dt=BF16: (nc.dram_tensor(name, shape, dt, kind="ExternalOutput").ap() if name in dbg else nc.dram_tensor(name, shape, dt).ap())
    Win = scr("Win_bf", [DEPTH, 25, 128, KC * 512])
    Wout = scr("Wout_bf", [DEPTH, 8, 128, KC * 512])
    Wuq = scr("Wuq_bf", [DEPTH, 8, 128, 12 * 512])
    Wukv = scr("Wukv_bf", [DEPTH, 8, 128, 4 * 512])
    pxT = scr("pxT", [1024, NT], F32)
    gpzT = scr("gpzT", [1024, NT], F32)
    qT = scr("qT", [1024, NT])
    kT = scr("kT", [1024, NT])
    ktok = scr("ktok", [NT, 1024])
    vtok = scr("vtok", [NT, 1024])
    goT = scr("goT", [1024, NT], F32)
    gmzT = scr("gmzT", [1024, NT], F32)
    qlatT = scr("qlatT", [1536, NT])
    kvlatT = scr("kvlatT", [512, NT])
    gazT = scr("gazT", [2048, NT], F32)
    gT = scr("gT", [16, NT], F32)
    krT = scr("krT", [64, NT], F32)
    krsT = scr("krsT", [64, NT], F32)
    QT = scr("QT", [16 * 192, NT])
    KVR = 4160
    kvrec = scr("kvrec", [KVR, NT])
    kvall = scr("kvall", [NCORES * KVR, NT])
    mixT = scr("mixT", [D, NT])
    x1 = scr("x1", [NT, D], F32)
    grow = scr("grow", [3, 16, NT], F32)
    drow = scr("drow", [8, 12, NT])
    Erec = scr("Erec", [16 * 256, 258], F32)
    Eall = scr("Eall", [NCORES * 16 * 256, 258], F32)
    hbuf = scr("hbuf", [2, 1024, NT], F32)
    edrec = scr("edrec", [1024, 32], F32)
    edall = scr("edall", [NCORES * 1024, 32], F32)

    with ExitStack() as top:
        fw = FW(nc, top)
        cc_key = fw.new_dsem("cc")

        import itertools
        uid = itertools.count()
        sbt = lambda st, name, shape, dt: st.enter_context(nc.sbuf_tensor("sb_" + name + "_%d" % next(uid), shape, dt))
        pst = lambda st, name, shape, dt: st.enter_context(nc.psum_tensor("ps_" + name + "_%d" % next(uid), shape, dt))
        ident = sbt(top, "ident", [128, 128], BF16)
        ones_bf = sbt(top, "ones_bf", [128, 128], BF16)
        pc = sbt(top, "pc", [128, DEPTH, NPC], F32)
        pcs = sbt(top, "pcs", [128, DEPTH, 8], F32)
        Bconst = fw.buf("const", dma=True)
        fw.dma("sp", ident[:], ident_in[:, :], Bconst, writes=[Bconst])
        for l in range(DEPTH):
            fw.dma("sp", pc[:, l, :], pcols[l, :, :], Bconst, writes=[Bconst])
        fw.op("dve", lambda e: e.memset(ones_bf[:], 1.0), writes=[Bconst])
        qscale = 192.0 ** -0.5
        for l in range(DEPTH):
            fw.op("dve", lambda e, l=l: e.tensor_scalar(out=pcs[:, l, 0:1], in0=pc[:, l, 64:65], scalar1=qscale, scalar2=None, op0=ALU.mult),
                  reads=[Bconst], writes=[Bconst])
            fw.op("dve", lambda e, l=l: e.tensor_scalar(out=pcs[:, l, 1:3], in0=pc[:, l, 66:68], scalar1=qscale, scalar2=None, op0=ALU.mult),
                  reads=[Bconst], writes=[Bconst])
        fw.barrier()

        def prep_weights(l):
            with ExitStack() as st:
                s32 = [sbt(st, "s32_%d" % i, [128, 8, 512], F32) for i in range(2)]
                s16 = [sbt(st, "s16_%d" % i, [128, KC, 512], BF16) for i in range(2)]
                B32 = [fw.buf("s32", dma=True) for _ in range(2)]
                B16 = [fw.buf("s16", dma=True) for _ in range(2)]
                for i in range(2):
                    fw.op("dve", lambda e, i=i: e.memset(s32[i][:], 0.0), writes=[B32[i]])
                cnt = {"ld": 0, "blk": 0, "cast": 0}

                def prep_block(src, nkc, pieces, gcol, scale, dst):
                    bi = cnt["blk"] % 2
                    cnt["blk"] += 1
                    for kg in range(0, nkc, 8):
                        nk = min(8, nkc - kg)
                        li = cnt["ld"] % 2
                        cnt["ld"] += 1
                        for (sc, w, do) in pieces:
                            fw.dma("sp", s32[li][:, 0:nk, do:do + w],
                                   src[kg * 128:(kg + nk) * 128, sc:sc + w].rearrange("(k p) c -> p k c", p=128),
                                   B32[li], writes=[B32[li]])
                        for k in range(nk):
                            ek = "dve" if cnt["cast"] % 2 == 0 else "pool"
                            cnt["cast"] += 1
                            if gcol is not None:
                                fn = lambda e, li=li, k=k, kk=kg + k: e.tensor_scalar(
                                    out=s16[bi][:, kk, :], in0=s32[li][:, k, :], scalar1=pc[:, l, gcol + kk:gcol + kk + 1],
                                    scalar2=float(scale), op0=ALU.mult, op1=ALU.mult)
                            else:
                                fn = lambda e, li=li, k=k, kk=kg + k: e.tensor_copy(out=s16[bi][:, kk, :], in_=s32[li][:, k, :])
                            fw.op(ek, fn, reads=[B32[li], Bconst], writes=[B16[bi]])
                    fw.dma("pool", dst, s16[bi][:, 0:nkc, :].rearrange("p k c -> p (k c)"), B16[bi], reads=[B16[bi]])

                for b, (pieces, scale) in enumerate(WIN_BLOCKS):
                    prep_block(w_in[l], KC, pieces, 0, scale, Win[l, b])
                for b in range(8):
                    prep_block(w_out[l], KC, [(b * 512, 512, 0)], None, 1.0, Wout[l, b])
                for b in range(8):
                    pieces = []
                    for hh in range(2):
                        h = 2 * b + hh
                        pieces += [(h * 192, 192, hh * 256), (h * 192 + 160, 32, hh * 256 + 192), (h * 192 + 128, 32, hh * 256 + 224)]
                    prep_block(w_uq[l], 12, pieces, 32, 1.0, Wuq[l, b])
                for b in range(4):
                    pieces = [((4 * b + hh) * 256, 128, hh * 128) for hh in range(4)]
                    prep_block(w_ukv[l], 4, pieces, 44, 1.0, Wukv[l, b])
                for b in range(4):
                    pieces = [((4 * b + hh) * 256 + 128, 128, hh * 128) for hh in range(4)]
                    prep_block(w_ukv[l], 4, pieces, 44, 1.0, Wukv[l, 4 + b])
                fw.barrier()

        def phase_a(l, xsrc):
            with ExitStack() as st:
                hT = sbt(st, "hT", [128, KC, 1024], BF16)
                wr = [sbt(st, "wr%d" % i, [128, KC, 512], BF16) for i in range(2)]
                xs = [sbt(st, "xs%d" % i, [128, D], F32) for i in range(2)]
                xn = sbt(st, "xn", [128, D], BF16)
                stat = sbt(st, "stat", [128, 4], F32)
                ost = [sbt(st, "ost%d" % i, [128, 512], F32) for i in range(4)]
                osb = [sbt(st, "osb%d" % i, [128, 512], BF16) for i in range(4)]
                pt = [pst(st, "pt%d" % i, [128, 8, 128], BF16) for i in range(2)]
                pa = [pst(st, "pa%d" % i, [128, 512], F32) for i in range(4)]
                BhT = fw.buf("hT")
                Bwr = [fw.buf("wr", dma=True) for _ in range(2)]
                Bxs = [fw.buf("xs", dma=True) for _ in range(2)]
                Bxn = fw.buf("xn")
                Bstat = fw.buf("stat")
                Bost = [fw.buf("ost", dma=True) for _ in range(4)]
                Bpt = [fw.buf("pt") for _ in range(2)]
                Bpa = [fw.buf("pa") for _ in range(4)]
                ctr = {"x": 0, "pt": 0, "w": 0, "pa": 0, "o": 0, "ev": 0}

                tiles = []
                cur = []
                for s in subs:
                    if cur and sum(n for _, n in cur) + s[1] > 1024:
                        tiles.append(cur)
                        cur = []
                    cur.append(s)
                if cur:
                    tiles.append(cur)

                for tile in tiles:
                    T0 = tile[0][0]
                    TN = sum(n for _, n in tile)
                    for bo in range(0, TN, 128):
                        xi = ctr["x"] % 2
                        ctr["x"] += 1
                        fw.dma("sp", xs[xi][:], xsrc[T0 + bo:T0 + bo + 128, :], Bxs[xi], writes=[Bxs[xi]])
                        fw.op("act", lambda e, xi=xi: e.activation(out=xn[:], in_=xs[xi][:], func=AF.Square, accum_out=stat[:, 0:1]),
                              reads=[Bxs[xi]], writes=[Bxn, Bstat])
                        fw.op("act", lambda e: e.activation(out=stat[:, 1:2], in_=stat[:, 0:1], func=AF.Ln, scale=1.0 / D, bias=EPS),
                              reads=[Bstat], writes=[Bstat])
                        fw.op("act", lambda e: e.activation(out=stat[:, 2:3], in_=stat[:, 1:2], func=AF.Exp, scale=-0.5),
                              reads=[Bstat], writes=[Bstat])
                        fw.op("dve", lambda e, xi=xi: e.tensor_scalar(out=xn[:], in0=xs[xi][:], scalar1=stat[:, 2:3], scalar2=None, op0=ALU.mult),
                              reads=[Bxs[xi], Bstat], writes=[Bxn])
                        for g4 in range(4):
                            pi = ctr["pt"] % 2
                            ctr["pt"] += 1
                            fw.mm([lambda pe, k=k, pi=pi, g4=g4: pe.transpose(out=pt[pi][:, k, :], in_=xn[:, (g4 * 8 + k) * 128:(g4 * 8 + k + 1) * 128], identity=ident[:])
                                   for k in range(8)], reads=[Bxn, Bconst], writes=[Bpt[pi]])
                            ek = "act" if g4 % 2 == 0 else "dve"
                            if ek == "act":
                                fn = lambda e, pi=pi, g4=g4, bo=bo: e.copy(out=hT[:, g4 * 8:(g4 + 1) * 8, bo:bo + 128], in_=pt[pi][:])
                            else:
                                fn = lambda e, pi=pi, g4=g4, bo=bo: e.tensor_copy(out=hT[:, g4 * 8:(g4 + 1) * 8, bo:bo + 128], in_=pt[pi][:])
                            fw.op(ek, fn, reads=[Bpt[pi]], writes=[BhT])

                    def evac(pi, M, n, post, dest_ap, use_bf):
                        oi = ctr["o"] % 4
                        ctr["o"] += 1
                        o = osb[oi] if use_bf else ost[oi]
                        if post == "silu":
                            fw.op("act", lambda e: e.activation(out=o[0:M, 0:n], in_=pa[pi][0:M, 0:n], func=AF.Silu), reads=[Bpa[pi]], writes=[Bost[oi]])
                        elif post == "sigmoid":
                            fw.op("act", lambda e: e.activation(out=o[0:M, 0:n], in_=pa[pi][0:M, 0:n], func=AF.Sigmoid), reads=[Bpa[pi]], writes=[Bost[oi]])
                        else:
                            ctr["ev"] += 1
                            if ctr["ev"] % 2 == 0:
                                fw.op("act", lambda e: e.copy(out=o[0:M, 0:n], in_=pa[pi][0:M, 0:n]), reads=[Bpa[pi]], writes=[Bost[oi]])
                            else:
                                fw.op("dve", lambda e: e.tensor_copy(out=o[0:M, 0:n], in_=pa[pi][0:M, 0:n]), reads=[Bpa[pi]], writes=[Bost[oi]])
                        fw.dma("pool", dest_ap, o[0:M, 0:n], Bost[oi], reads=[Bost[oi]])

                    for b, spec in enumerate(A_SPECS):
                        wi = ctr["w"] % 2
                        ctr["w"] += 1
                        fw.dma("sp", wr[wi][:].rearrange("p k c -> p (k c)"), Win[l, b], Bwr[wi], writes=[Bwr[wi]])
                        kind = spec[0]
                        if kind == "T":
                            dest = {"ktok": ktok, "vtok": vtok}[spec[1]]
                            c0 = spec[2]
                            for bo in range(0, TN, 128):
                                pi = ctr["pa"] % 4
                                ctr["pa"] += 1
                                fw.mm([lambda pe, kc=kc, pi=pi, bo=bo, wi=wi: pe.matmul(pa[pi][:, :], lhsT=hT[:, kc, bo:bo + 128], rhs=wr[wi][:, kc, :],
                                                                                         start=(kc == 0), stop=(kc == KC - 1)) for kc in range(KC)],
                                      reads=[BhT, Bwr[wi]], writes=[Bpa[pi]])
                                evac(pi, 128, 512, "copy", dest[T0 + bo:T0 + bo + 128, c0:c0 + 512], True)
                        else:
                            for (off, M, dname, r0, post) in spec[1]:
                                dest, use_bf = {"pxT": (pxT, False), "gpzT": (gpzT, False), "qT": (qT, True), "kT": (kT, True),
                                                "goT": (goT, False), "gmzT": (gmzT, False), "qlatT": (qlatT, True), "kvlatT": (kvlatT, True),
                                                "gazT": (gazT, False), "gT": (gT, False), "krT": (krT, False), "krsT": (krsT, False)}[dname]
                                so = 0
                                for (t0, n) in tile:
                                    pi = ctr["pa"] % 4
                                    ctr["pa"] += 1
                                    fw.mm([lambda pe, kc=kc, pi=pi, so=so, n=n, wi=wi, off=off, M=M: pe.matmul(
                                        pa[pi][0:M, 0:n], lhsT=wr[wi][:, kc, off:off + M], rhs=hT[:, kc, so:so + n],
                                        start=(kc == 0), stop=(kc == KC - 1)) for kc in range(KC)],
                                        reads=[BhT, Bwr[wi]], writes=[Bpa[pi]])
                                    evac(pi, M, n, post, dest[r0:r0 + M, t0:t0 + n], use_bf)
                                    so += n
                fw.barrier()

        def phase_f(l, xsrc, ydst):
            with ExitStack() as st:
                mt = sbt(st, "mt", [128, KC, 512], BF16)
                wr = [sbt(st, "fwr%d" % i, [128, KC, 512], BF16) for i in range(2)]
                xr = [sbt(st, "fxr%d" % i, [128, 512], F32) for i in range(4)]
                pa = [pst(st, "fpa%d" % i, [128, 512], F32) for i in range(4)]
                Bmt = fw.buf("mt", dma=True)
                Bwr = [fw.buf("fwr", dma=True) for _ in range(2)]
                Bxr = [fw.buf("fxr", dma=True) for _ in range(4)]
                Bpa = [fw.buf("fpa") for _ in range(4)]
                ctr = {"w": 0, "x": 0, "pa": 0}
                for (t0, n) in subs:
                    fw.dma("sp", mt[:, :, 0:n], mixT[:, t0:t0 + n].rearrange("(k p) t -> p k t", p=128), Bmt, writes=[Bmt])
                    for b in range(8):
                        wi = ctr["w"] % 2
                        ctr["w"] += 1
                        fw.dma("sp", wr[wi][:].rearrange("p k c -> p (k c)"), Wout[l, b], Bwr[wi], writes=[Bwr[wi]])
                        for bo in range(0, n, 128):
                            xi = ctr["x"] % 4
                            ctr["x"] += 1
                            pi = ctr["pa"] % 4
                            ctr["pa"] += 1
                            fw.dma("sp", xr[xi][:], xsrc[t0 + bo:t0 + bo + 128, b * 512:(b + 1) * 512], Bxr[xi], writes=[Bxr[xi]])
                            fw.mm([lambda pe, kc=kc, pi=pi, bo=bo, wi=wi: pe.matmul(pa[pi][:, :], lhsT=mt[:, kc, bo:bo + 128], rhs=wr[wi][:, kc, :],
                                                                                     start=(kc == 0), stop=(kc == KC - 1)) for kc in range(KC)],
                                  reads=[Bmt, Bwr[wi]], writes=[Bpa[pi]])
                            fw.op("dve", lambda e, xi=xi, pi=pi: e.tensor_tensor(out=xr[xi][:], in0=pa[pi][:], in1=xr[xi][:], op=ALU.add),
                                  reads=[Bpa[pi], Bxr[xi]], writes=[Bxr[xi]])
                            fw.dma("pool", ydst[t0 + bo:t0 + bo + 128, b * 512:(b + 1) * 512], xr[xi][:], Bxr[xi], reads=[Bxr[xi]])
                fw.barrier()


        class Ring:
            def __init__(self, st, name, n, shape, dt, psum=False, dma=False):
                mk = pst if psum else sbt
                self.t = [mk(st, "%s%d" % (name, i), shape, dt) for i in range(n)]
                self.b = [fw.buf(name, dma=dma) for _ in range(n)]
                self.i = 0

            def next(self):
                k = self.i % len(self.t)
                self.i += 1
                return self.t[k], self.b[k]

        def rstd_from(st_ring, src_ap, M, n, scale, reads):
            t1, b1 = st_ring.next()
            fw.op("act", lambda e: e.activation(out=t1[0:M, 0:n], in_=src_ap, func=AF.Ln, scale=scale, bias=EPS), reads=reads, writes=[b1])
            t2, b2 = st_ring.next()
            fw.op("act", lambda e: e.activation(out=t2[0:M, 0:n], in_=t1[0:M, 0:n], func=AF.Exp, scale=-0.5), reads=[b1], writes=[b2])
            return t2, b2

        def phase_b(l):
            with ExitStack() as st:
                ql = sbt(st, "ql", [128, 12, 512], BF16)
                kvl = sbt(st, "kvl", [128, 4, 512], BF16)
                sq = sbt(st, "sq", [128, 12, 512], BF16)
                Bql = fw.buf("ql", dma=True); Bkvl = fw.buf("kvl", dma=True); Bsq = fw.buf("sq")
                wq = Ring(st, "wq", 2, [128, 12, 512], BF16, dma=True)
                wk = Ring(st, "wk", 2, [128, 4, 512], BF16, dma=True)
                rope = sbt(st, "rope", [64, 4, 512], F32)
                Brope = fw.buf("rope", dma=True)
                keep = sbt(st, "keep", [128, 4, 512], F32)
                Bkeep = fw.buf("keep")
                rkvc = sbt(st, "rkvc", [128, 4], F32)
                Brkvc = fw.buf("rkvc")
                f32r = Ring(st, "bf32", 8, [128, 512], F32)
                b16r = Ring(st, "bb16", 6, [128, 512], BF16, dma=True)
                psr = Ring(st, "bps", 7, [128, 512], F32, psum=True)
                psc = pst(st, "bpsc", [128, 8], F32)
                Bpsc = fw.buf("psc")
                for (t0, n) in subs:
                    fw.dma("sp", ql[:, :, 0:n], qlatT[:, t0:t0 + n].rearrange("(k p) t -> p k t", p=128), Bql, writes=[Bql])
                    fw.dma("sp", kvl[:, :, 0:n], kvlatT[:, t0:t0 + n].rearrange("(k p) t -> p k t", p=128), Bkvl, writes=[Bkvl])
                    for i, src in enumerate((krT, krsT, cosT, sinT)):
                        fw.dma("sp", rope[:, i, 0:n], src[:, t0:t0 + n], Brope, writes=[Brope])
                    for (lat, Blat, nk, kbase, dim) in ((ql, Bql, 12, 0, 1536.0), (kvl, Bkvl, 4, 2, 512.0)):
                        fw.op("dve", lambda e, lat=lat, nk=nk: e.tensor_tensor(out=sq[:, 0:nk, 0:n], in0=lat[:, 0:nk, 0:n], in1=lat[:, 0:nk, 0:n], op=ALU.mult),
                              reads=[Blat], writes=[Bsq])
                        p, bp = psr.next()
                        fw.mm([lambda pe, j=j, p=p, nk=nk: pe.matmul(p[:, 0:n], lhsT=ones_bf[:, :], rhs=sq[:, j, 0:n], start=(j == 0), stop=(j == nk - 1))
                               for j in range(nk)], reads=[Bsq, Bconst], writes=[bp])
                        r, br = rstd_from(f32r, p[:, 0:n], 128, n, 1.0 / dim, [bp])
                        fw.op("dve", lambda e, r=r, kbase=kbase: e.tensor_copy(out=keep[:, kbase, 0:n], in_=r[:, 0:n]), reads=[br], writes=[Bkeep])
                        fw.op("dve", lambda e, r=r, kbase=kbase: e.tensor_tensor(out=keep[:, kbase + 1, 0:n], in0=r[:, 0:n], in1=r[:, 0:n], op=ALU.mult),
                              reads=[br], writes=[Bkeep])
                        if lat is kvl:
                            for bi in range(n // 128):
                                fw.mm([lambda pe, j=j, bi=bi: pe.matmul(psc[:, bi:bi + 1], lhsT=sq[:, j, bi * 128:(bi + 1) * 128], rhs=ones_bf[:, 0:1],
                                                                        start=(j == 0), stop=(j == 3)) for j in range(4)], reads=[Bsq, Bconst], writes=[Bpsc])
                            nb_ = n // 128
                            t1, b1 = f32r.next()
                            fw.op("act", lambda e, t1=t1: e.activation(out=t1[:, 0:nb_], in_=psc[:, 0:nb_], func=AF.Ln, scale=1.0 / 512, bias=EPS), reads=[Bpsc], writes=[b1])
                            fw.op("act", lambda e, t1=t1: e.activation(out=rkvc[:, 0:nb_], in_=t1[:, 0:nb_], func=AF.Exp, scale=-0.5), reads=[b1], writes=[Brkvc])
                    rq, rq2, rkv, rkv2 = keep[:, 0, :], keep[:, 1, :], keep[:, 2, :], keep[:, 3, :]

                    def normed(P, bP, M, rr, rr2, dim, gcol_ap, extra=None):
                        s16, bs16 = b16r.next()
                        fw.op("act", lambda e: e.activation(out=s16[0:M, 0:n], in_=P[0:M, 0:n], func=AF.Square), reads=[bP], writes=[bs16])
                        p4, bp4 = psr.next()
                        fw.mm([lambda pe: pe.matmul(p4[:, 0:n], lhsT=ones_bf[0:M, :], rhs=s16[0:M, 0:n], start=True, stop=True)], reads=[bs16, Bconst], writes=[bp4])
                        u, bu = f32r.next()
                        fw.op("dve", lambda e: e.tensor_tensor(out=u[0:M, 0:n], in0=p4[0:M, 0:n], in1=rr2[0:M, 0:n], op=ALU.mult), reads=[bp4, Bkeep], writes=[bu])
                        w, bw = rstd_from(f32r, u[0:M, 0:n], M, n, 1.0 / dim, [bu])
                        f, bf = f32r.next()
                        fw.op("dve", lambda e: e.tensor_tensor(out=f[0:M, 0:n], in0=w[0:M, 0:n], in1=rr[0:M, 0:n], op=ALU.mult), reads=[bw, Bkeep], writes=[bf])
                        return f, bf

                    def rope_combine(Pa, bPa, Pb, bPb, f, bf, ga, gb, dest_ap):
                        a, ba = f32r.next()
                        fw.op("dve", lambda e: e.scalar_tensor_tensor(out=a[0:64, 0:n], in0=Pa, scalar=ga, in1=f[0:64, 0:n], op0=ALU.mult, op1=ALU.mult),
                              reads=[bPa, bf, Bconst], writes=[ba])
                        b_, bb = f32r.next()
                        fw.op("dve", lambda e: e.scalar_tensor_tensor(out=b_[0:64, 0:n], in0=Pb, scalar=gb, in1=f[0:64, 0:n], op0=ALU.mult, op1=ALU.mult),
                              reads=[bPb, bf, Bconst], writes=[bb])
                        fw.op("dve", lambda e: e.tensor_tensor(out=a[0:64, 0:n], in0=a[0:64, 0:n], in1=rope[:, 2, 0:n], op=ALU.mult), reads=[ba, Brope], writes=[ba])
                        fw.op("dve", lambda e: e.tensor_tensor(out=b_[0:64, 0:n], in0=b_[0:64, 0:n], in1=rope[:, 3, 0:n], op=ALU.mult), reads=[bb, Brope], writes=[bb])
                        o, bo = b16r.next()
                        fw.op("dve", lambda e: e.tensor_tensor(out=o[0:64, 0:n], in0=a[0:64, 0:n], in1=b_[0:64, 0:n], op=ALU.add), reads=[ba, bb], writes=[bo])
                        fw.dma("pool", dest_ap, o[0:64, 0:n], bo, reads=[bo])

                    s16, bs16 = b16r.next()
                    fw.op("dve", lambda e: e.tensor_tensor(out=s16[0:64, 0:n], in0=rope[:, 0, 0:n], in1=rope[:, 0, 0:n], op=ALU.mult), reads=[Brope], writes=[bs16])
                    p4, bp4 = psr.next()
                    fw.mm([lambda pe: pe.matmul(p4[:, 0:n], lhsT=ones_bf[0:64, :], rhs=s16[0:64, 0:n], start=True, stop=True)], reads=[bs16, Bconst], writes=[bp4])
                    fk, bfk = rstd_from(f32r, p4[0:64, 0:n], 64, n, 1.0 / 64, [bp4])
                    rope_combine(rope[:, 0, 0:n], Brope, rope[:, 1, 0:n], Brope, fk, bfk, pc[0:64, l, 68:69], pc[0:64, l, 69:70], kvrec[2048:2112, t0:t0 + n])

                    for h in range(16):
                        if h % 2 == 0:
                            wqt, bwq = wq.next()
                            fw.dma("sp", wqt[:].rearrange("p k c -> p (k c)"), Wuq[l, h // 2], bwq, writes=[bwq])
                        if h % 4 == 0:
                            wkt, bwk = wk.next()
                            fw.dma("sp", wkt[:].rearrange("p k c -> p (k c)"), Wukv[l, h // 4], bwk, writes=[bwk])
                        c0 = (h % 2) * 256
                        P1, bP1 = psr.next()
                        fw.mm([lambda pe, j=j: pe.matmul(P1[:, 0:n], lhsT=wqt[:, j, c0:c0 + 128], rhs=ql[:, j, 0:n], start=(j == 0), stop=(j == 11)) for j in range(12)],
                              reads=[bwq, Bql], writes=[bP1])
                        P2, bP2 = psr.next()
                        fw.mm([lambda pe, j=j: pe.matmul(P2[0:64, 0:n], lhsT=wqt[:, j, c0 + 128:c0 + 192], rhs=ql[:, j, 0:n], start=(j == 0), stop=(j == 11)) for j in range(12)],
                              reads=[bwq, Bql], writes=[bP2])
                        P3, bP3 = psr.next()
                        fw.mm([lambda pe, j=j: pe.matmul(P3[0:64, 0:n], lhsT=wqt[:, j, c0 + 192:c0 + 256], rhs=ql[:, j, 0:n], start=(j == 0), stop=(j == 11)) for j in range(12)],
                              reads=[bwq, Bql], writes=[bP3])
                        f, bf = normed(P1, bP1, 128, rq, rq2, 128.0, None)
                        o, bo = b16r.next()
                        fw.op("dve", lambda e: e.scalar_tensor_tensor(out=o[:, 0:n], in0=P1[:, 0:n], scalar=pcs[:, l, 0:1], in1=f[:, 0:n], op0=ALU.mult, op1=ALU.mult),
                              reads=[bP1, bf, Bconst], writes=[bo])
                        fw.dma("pool", QT[h * 192:h * 192 + 128, t0:t0 + n], o[:, 0:n], bo, reads=[bo])
                        f2, bf2 = normed(P2, bP2, 64, rq, rq2, 64.0, None)
                        rope_combine(P2[0:64, 0:n], bP2, P3[0:64, 0:n], bP3, f2, bf2, pcs[0:64, l, 1:2], pcs[0:64, l, 2:3], QT[h * 192 + 128:h * 192 + 192, t0:t0 + n])
                        k0 = (h % 4) * 128
                        P5, bP5 = psr.next()
                        fw.mm([lambda pe, j=j: pe.matmul(P5[:, 0:n], lhsT=wkt[:, j, k0:k0 + 128], rhs=kvl[:, j, 0:n], start=(j == 0), stop=(j == 3)) for j in range(4)],
                              reads=[bwk, Bkvl], writes=[bP5])
                        f3, bf3 = normed(P5, bP5, 128, rkv, rkv2, 128.0, None)
                        o, bo = b16r.next()
                        fw.op("dve", lambda e: e.scalar_tensor_tensor(out=o[:, 0:n], in0=P5[:, 0:n], scalar=pc[:, l, 65:66], in1=f3[:, 0:n], op0=ALU.mult, op1=ALU.mult),
                              reads=[bP5, bf3, Bconst], writes=[bo])
                        fw.dma("pool", kvrec[h * 128:(h + 1) * 128, t0:t0 + n], o[:, 0:n], bo, reads=[bo])
                    for vb in range(4):
                        wkt, bwk = wk.next()
                        fw.dma("sp", wkt[:].rearrange("p k c -> p (k c)"), Wukv[l, 4 + vb], bwk, writes=[bwk])
                        for bi in range(n // 128):
                            P7, bP7 = psr.next()
                            fw.mm([lambda pe, j=j: pe.matmul(P7[:, :], lhsT=kvl[:, j, bi * 128:(bi + 1) * 128], rhs=wkt[:, j, :], start=(j == 0), stop=(j == 3)) for j in range(4)],
                                  reads=[bwk, Bkvl], writes=[bP7])
                            o, bo = b16r.next()
                            fw.op("dve", lambda e: e.tensor_scalar(out=o[:, :], in0=P7[:, :], scalar1=rkvc[:, bi:bi + 1], scalar2=None, op0=ALU.mult),
                                  reads=[bP7, Brkvc], writes=[bo])
                            tb = (t0 + bi * 128)
                            fw.dma("pool", kvrec[2112 + vb * 512:2112 + (vb + 1) * 512, tb:tb + 128].rearrange("(hh p) v -> p hh v", p=128),
                                   o[:, :].rearrange("p (hh v) -> p hh v", hh=4), bo, reads=[bo])
                fw.barrier()

        def phase_c(l):
            with ExitStack() as st:
                nmax = max(np_, ns)
                Kn = Ring(st, "Kn", 2, [128, NCORES, nmax], BF16, dma=True)
                Vh = Ring(st, "Vh", 2, [128, NCORES, nmax], BF16, dma=True)
                Kpe = sbt(st, "Kpe", [64, NCORES, nmax], BF16)
                BKpe = fw.buf("Kpe", dma=True)
                Qn = Ring(st, "Qn", 2, [128, 512], BF16, dma=True)
                Qp = Ring(st, "Qp", 2, [64, 512], BF16, dma=True)
                gz = Ring(st, "gz", 2, [128, 512], F32, dma=True)
                PT = Ring(st, "PT", 3, [128, 512], BF16)
                tmp = Ring(st, "ctmp", 4, [128, 512], F32)
                ob = Ring(st, "cob", 2, [128, 512], BF16, dma=True)
                STr = Ring(st, "ST", 4, [128, 512], F32, psum=True)
                Or = Ring(st, "O", 2, [128, 512], F32, psum=True)
                Lr = Ring(st, "L", 2, [128, 512], F32, psum=True)
                kva = kvall.rearrange("(c r) t -> r c t", c=NCORES)
                for (s0, nseg) in ((0, np_), (np_, ns)):
                    nblk = nseg // 128
                    fw.dma("sp", Kpe[:, :, 0:nseg], kva[2048:2112, :, s0:s0 + nseg], BKpe, writes=[BKpe])
                    qtiles = [(t0, n) for (t0, n) in subs if s0 <= t0 < s0 + nseg]
                    for h in range(16):
                        knt, bkn = Kn.next()
                        fw.dma("sp", knt[:, :, 0:nseg], kva[h * 128:(h + 1) * 128, :, s0:s0 + nseg], bkn, writes=[bkn])
                        vht, bvh = Vh.next()
                        fw.dma("sp", vht[:, :, 0:nseg], kva[2112 + h * 128:2112 + (h + 1) * 128, :, s0:s0 + nseg], bvh, writes=[bvh])
                        for (t0, n) in qtiles:
                            qn, bqn = Qn.next()
                            fw.dma("sp", qn[:, 0:n], QT[h * 192:h * 192 + 128, t0:t0 + n], bqn, writes=[bqn])
                            qp, bqp = Qp.next()
                            fw.dma("sp", qp[:, 0:n], QT[h * 192 + 128:h * 192 + 192, t0:t0 + n], bqp, writes=[bqp])
                            g, bg = gz.next()
                            fw.dma("sp", g[:, 0:n], gazT[h * 128:(h + 1) * 128, t0:t0 + n], bg, writes=[bg])
                            O, bO = Or.next()
                            L, bL = Lr.next()
                            nkb = NCORES * nblk
                            def issue_S(kb):
                                c, blk = kb // nblk, kb % nblk
                                S, bS = STr.next()
                                fw.mm([lambda pe: pe.matmul(S[:, 0:n], lhsT=knt[:, c, blk * 128:(blk + 1) * 128], rhs=qn[:, 0:n], start=True, stop=False),
                                       lambda pe: pe.matmul(S[:, 0:n], lhsT=Kpe[:, c, blk * 128:(blk + 1) * 128], rhs=qp[:, 0:n], start=False, stop=True)],
                                      reads=[bkn, BKpe, bqn, bqp], writes=[bS])
                                return S, bS

                            LOOK = 2
                            pend = [issue_S(k) for k in range(min(LOOK, nkb))]
                            for kb in range(nkb):
                                c, blk = kb // nblk, kb % nblk
                                S, bS = pend.pop(0)
                                if kb + LOOK < nkb:
                                    pend.append(issue_S(kb + LOOK))
                                p, bp = PT.next()
                                fw.op("act", lambda e: e.activation(out=p[:, 0:n], in_=S[:, 0:n], func=AF.Exp), reads=[bS], writes=[bp])
                                fw.mm([lambda pe: pe.matmul(O[:, 0:n], lhsT=vht[:, c, blk * 128:(blk + 1) * 128], rhs=p[:, 0:n], start=(kb == 0), stop=(kb == nkb - 1)),
                                       lambda pe: pe.matmul(L[:, 0:n], lhsT=ones_bf[:, :], rhs=p[:, 0:n], start=(kb == 0), stop=(kb == nkb - 1))],
                                      reads=[bvh, bp, Bconst], writes=[bO, bL])
                            rl, brl = tmp.next()
                            fw.op("dve", lambda e: e.reciprocal(out=rl[:, 0:n], in_=L[:, 0:n]), reads=[bL], writes=[brl])
                            o1, bo1 = tmp.next()
                            fw.op("dve", lambda e: e.tensor_tensor(out=o1[:, 0:n], in0=O[:, 0:n], in1=rl[:, 0:n], op=ALU.mult), reads=[bO, brl], writes=[bo1])
                            o2, bo2 = ob.next()
                            fw.op("dve", lambda e: e.tensor_tensor(out=o2[:, 0:n], in0=o1[:, 0:n], in1=g[:, 0:n], op=ALU.mult), reads=[bo1, bg], writes=[bo2])
                            fw.dma("pool", mixT[2048 + h * 128:2048 + (h + 1) * 128, t0:t0 + n], o2[:, 0:n], bo2, reads=[bo2])
                fw.barrier()


        def phase_e_edges(l):
            with ExitStack() as st:
                et = sbt(st, "et", [128, 8, 32], F32)
                Bet = fw.buf("et", dma=True)
                for si, (s0, nseg) in enumerate(((0, np_), (np_, ns))):
                    fw.dma("sp", et[:, :, si * 16:si * 16 + 8], pxT[:, s0:s0 + 8].rearrange("(j p) e -> p j e", p=128), Bet, writes=[Bet])
                    fw.dma("sp", et[:, :, si * 16 + 8:si * 16 + 16], pxT[:, s0 + nseg - 8:s0 + nseg].rearrange("(j p) e -> p j e", p=128), Bet, writes=[Bet])
                fw.dma("sp", edrec.rearrange("(j p) e -> p j e", p=128), et[:], Bet, reads=[Bet])
                fw.barrier()

        def phase_e(l):
            with ExitStack() as st:
                ea = sbt(st, "ea", [128, NCORES, 8, 32], F32)
                Bea = fw.buf("ea", dma=True)
                halo = sbt(st, "halo", [128, 8, 32], F32)
                Bhalo = fw.buf("halo")
                sel = sbt(st, "sel", [128, 16], F32)
                pcr = sbt(st, "pcr", [128, 4, 2, 16], F32)
                Bsel = fw.buf("sel", dma=True)
                pw32 = sbt(st, "pw32", [128, 8, 256], F32)
                pw = sbt(st, "pw", [128, 8, 256], BF16)
                Bpw = fw.buf("pw", dma=True)
                nmax = max(np_, ns)
                P = Ring(st, "P", 2, [128, nmax + 16], F32, dma=True)
                A = Ring(st, "A", 3, [128, nmax + 16], F32)
                df = [sbt(st, "df%d" % i, [128, nmax], BF16) for i in range(2)]
                Bdf = [fw.buf("df") for _ in range(2)]
                gz = Ring(st, "egz", 2, [128, 512], F32, dma=True)
                ob = Ring(st, "eob", 2, [128, 512], BF16, dma=True)
                pp = Ring(st, "epp", 2, [128, 512], F32, psum=True)
                for c in range(NCORES):
                    fw.dma("sp", ea[:, c, :, :], edall[c * 1024:(c + 1) * 1024, :].rearrange("(j p) e -> p j e", p=128), Bea, writes=[Bea])
                fw.dma("sp", sel[:], sel_in[:, :], Bsel, writes=[Bsel])
                fw.dma("sp", pcr[:].rearrange("p g s e -> p (g s e)"), pcorr_in[:, :], Bsel, writes=[Bsel])
                fw.dma("sp", pw32[:], pool_w[l].rearrange("g (k p) d -> p (g k) d", p=128), Bpw, writes=[Bpw])
                fw.op("dve", lambda e: e.tensor_copy(out=pw[:], in_=pw32[:]), reads=[Bpw], writes=[Bpw])
                fw.op("dve", lambda e: e.memset(halo[:], 0.0), writes=[Bhalo])
                for si in range(2):
                    for side in range(2):
                        dst = halo[:, :, si * 16 + side * 8:si * 16 + side * 8 + 8]
                        for c in range(NCORES):
                            src = ea[:, c, :, si * 16 + (8 if side == 0 else 0):si * 16 + (16 if side == 0 else 8)]
                            fw.op("dve", lambda e, dst=dst, src=src, c=c, side=side: e.scalar_tensor_tensor(
                                out=dst, in0=src, scalar=sel[:, side * 8 + c:side * 8 + c + 1], in1=dst, op0=ALU.mult, op1=ALU.add),
                                reads=[Bea, Bsel, Bhalo], writes=[Bhalo])
                for si, (s0, nseg) in enumerate(((0, np_), (np_, ns))):
                    qtiles = [(t0, n) for (t0, n) in subs if s0 <= t0 < s0 + nseg]
                    for g in range(4):
                        w = 2 << g
                        for jj in range(2):
                            j = 2 * g + jj
                            Pt, bP = P.next()
                            fw.dma("sp", Pt[:, 8:8 + nseg], pxT[j * 128:(j + 1) * 128, s0:s0 + nseg], bP, writes=[bP])
                            fw.op("dve", lambda e: e.tensor_copy(out=Pt[:, 0:8], in_=halo[:, j, si * 16:si * 16 + 8]), reads=[Bhalo], writes=[bP])
                            fw.op("dve", lambda e: e.tensor_copy(out=Pt[:, 8 + nseg:16 + nseg], in_=halo[:, j, si * 16 + 8:si * 16 + 16]), reads=[Bhalo], writes=[bP])
                            NN = nseg + 16
                            cur, bcur = A.next()
                            fw.op("dve", lambda e: e.tensor_tensor(out=cur[:, 1:NN], in0=Pt[:, 0:NN - 1], in1=Pt[:, 1:NN], op=ALU.add), reads=[bP], writes=[bcur])
                            lo, hi = 1, NN
                            sh = 1
                            for lev in range(g):
                                nxt, bnxt = A.next()
                                fw.op("dve", lambda e, cur=cur, nxt=nxt, lo=lo, hi=hi, sh=sh: e.tensor_tensor(
                                    out=nxt[:, lo + sh:hi - sh], in0=cur[:, lo:hi - 2 * sh], in1=cur[:, lo + 2 * sh:hi], op=ALU.add), reads=[bcur], writes=[bnxt])
                                cur, bcur = nxt, bnxt
                                lo, hi = lo + sh, hi - sh
                                sh *= 2
                            fw.op("dve", lambda e: e.tensor_tensor(out=cur[:, 8:16], in0=cur[:, 8:16], in1=pcr[:, g, si, 0:8], op=ALU.mult), reads=[bcur, Bsel], writes=[bcur])
                            fw.op("dve", lambda e: e.tensor_tensor(out=cur[:, nseg:nseg + 8], in0=cur[:, nseg:nseg + 8], in1=pcr[:, g, si, 8:16], op=ALU.mult),
                                  reads=[bcur, Bsel], writes=[bcur])
                            fw.op("dve", lambda e: e.scalar_tensor_tensor(out=df[jj][:, 0:nseg], in0=cur[:, 8:8 + nseg], scalar=1.0 / w, in1=Pt[:, 8:8 + nseg],
                                                                          op0=ALU.mult, op1=ALU.subtract), reads=[bcur, bP], writes=[Bdf[jj]])
                        for dd in range(2):
                            jo = 2 * g + dd
                            for (t0, n) in qtiles:
                                lo_ = t0 - s0
                                pz, bpz = gz.next()
                                fw.dma("sp", pz[:, 0:n], gpzT[jo * 128:(jo + 1) * 128, t0:t0 + n], bpz, writes=[bpz])
                                pq, bpq = pp.next()
                                fw.mm([lambda pe, kk=kk: pe.matmul(pq[:, 0:n], lhsT=pw[:, g * 2 + kk, dd * 128:(dd + 1) * 128], rhs=df[kk][:, lo_:lo_ + n],
                                                                   start=(kk == 0), stop=(kk == 1)) for kk in range(2)], reads=[Bpw, Bdf[0], Bdf[1]], writes=[bpq])
                                o, bo = ob.next()
                                fw.op("dve", lambda e: e.scalar_tensor_tensor(out=o[:, 0:n], in0=pq[:, 0:n], scalar=pc[:, l, 48 + jo:49 + jo], in1=pz[:, 0:n],
                                                                              op0=ALU.mult, op1=ALU.mult), reads=[bpq, bpz, Bconst], writes=[bo])
                                fw.dma("pool", mixT[jo * 128:(jo + 1) * 128, t0:t0 + n], o[:, 0:n], bo, reads=[bo])
                fw.barrier()


        segs = ((0, np_), (np_, ns))

        def phase_d0(l):
            with ExitStack() as st:
                g = sbt(st, "dg", [16, NT], F32); e1 = sbt(st, "de1", [16, NT], F32); ls = sbt(st, "dls", [16, NT], F32)
                on = sbt(st, "don", [16, NT], F32); Bc = sbt(st, "dB", [16, NT], F32)
                Bg = fw.buf("dg", dma=True); Be = fw.buf("de"); Bl = fw.buf("dl", dma=True); Bo = fw.buf("do"); BB = fw.buf("dB", dma=True)
                fw.dma("sp", g[:], gT[:, :], Bg, writes=[Bg])
                fw.op("dve", lambda e: e.memset(on[:], 1.0), writes=[Bo])
                fw.op("dve", lambda e: e.tensor_scalar(out=g[:], in0=g[:], scalar1=pc[0:16, l, 70:71], scalar2=None, op0=ALU.add), reads=[Bg, Bconst], writes=[Bg])
                fw.op("act", lambda e: e.activation(out=e1[:], in_=g[:], func=AF.Exp, scale=-1.0), reads=[Bg], writes=[Be])
                fw.op("act", lambda e: e.activation(out=ls[:], in_=e1[:], func=AF.Ln, bias=1.0), reads=[Be], writes=[Bl])
                fw.op("dve", lambda e: e.tensor_scalar(out=ls[:], in0=ls[:], scalar1=-1.0, scalar2=None, op0=ALU.mult), reads=[Bl], writes=[Bl])
                for (s0, nseg) in segs:
                    fw.op("dve", lambda e: e.tensor_tensor_scan(out=Bc[:, s0:s0 + nseg], data0=on[:, s0:s0 + nseg], data1=ls[:, s0:s0 + nseg], initial=0.0,
                                                                op0=ALU.mult, op1=ALU.add), reads=[Bo, Bl], writes=[BB])
                fw.dma("pool", grow[0], g[:], Bg, reads=[Bg])
                fw.dma("pool", grow[1], Bc[:], BB, reads=[BB])
                fw.dma("pool", grow[2], ls[:], Bl, reads=[Bl])
                fw.barrier()

        def phase_d1(l):
            with ExitStack() as st:
                tl = [sbt(st, "d1t%d" % i, [4, NT], F32) for i in range(8)]
                bt = [fw.buf("d1t", dma=True) for _ in range(8)]
                hb = [sbt(st, "d1h%d" % i, [4, NT], BF16) for i in range(3)]
                bh = [fw.buf("d1h", dma=True) for _ in range(3)]
                I, Bq, Lq, Aq, bq, aE, dq, rr = tl
                bI, bB, bL, bA, bb, baE, bd, brr = bt

                def split_store(x, bx, dirn, r0):
                    cur, bcur = x, bx
                    for part in range(3):
                        fw.op("dve", lambda e: e.tensor_copy(out=hb[part][:], in_=cur[:]), reads=[bcur], writes=[bh[part]])
                        fw.dma("pool", drow[dirn * 4:dirn * 4 + 4, r0 + part, :], hb[part][:], bh[part], reads=[bh[part]])
                        if part < 2:
                            fw.op("dve", lambda e: e.tensor_tensor(out=rr[:], in0=cur[:], in1=hb[part][:], op=ALU.subtract), reads=[bcur, bh[part], brr], writes=[brr])
                            cur, bcur = rr, brr

                fw.dma("sp", I[:], grow[0, 0:4, :], bI, writes=[bI])
                fw.dma("sp", Bq[:], grow[1, 4:8, :], bB, writes=[bB])
                fw.op("dve", lambda e: e.tensor_tensor(out=Aq[:], in0=I[:], in1=Bq[:], op=ALU.subtract), reads=[bI, bB], writes=[bA])
                for (s0, nseg) in segs:
                    fw.op("dve", lambda e: e.tensor_scalar(out=aE[:, s0:s0 + nseg], in0=Aq[:, s0:s0 + nseg], scalar1=Bq[:, s0 + nseg - 1:s0 + nseg], scalar2=None, op0=ALU.add),
                          reads=[bA, bB], writes=[baE])
                split_store(Aq, bA, 0, 0)
                split_store(Bq, bB, 0, 3)
                split_store(aE, baE, 0, 6)
                split_store(Bq, bB, 0, 9)
                fw.dma("sp", I[:], grow[0, 8:12, :], bI, writes=[bI])
                fw.dma("sp", Bq[:], grow[1, 12:16, :], bB, writes=[bB])
                fw.dma("sp", Lq[:], grow[2, 12:16, :], bL, writes=[bL])
                fw.op("dve", lambda e: e.tensor_tensor(out=Lq[:], in0=Bq[:], in1=Lq[:], op=ALU.subtract), reads=[bB, bL], writes=[bL])
                fw.op("dve", lambda e: e.tensor_tensor(out=Aq[:], in0=I[:], in1=Lq[:], op=ALU.add), reads=[bI, bL], writes=[bA])
                fw.op("dve", lambda e: e.tensor_scalar(out=bq[:], in0=Lq[:], scalar1=-1.0, scalar2=None, op0=ALU.mult), reads=[bL], writes=[bb])
                for (s0, nseg) in segs:
                    fw.op("dve", lambda e: e.tensor_scalar(out=dq[:, s0:s0 + nseg], in0=bq[:, s0:s0 + nseg], scalar1=Bq[:, s0 + nseg - 1:s0 + nseg], scalar2=None, op0=ALU.add),
                          reads=[bb, bB], writes=[bd])
                split_store(Aq, bA, 1, 0)
                split_store(bq, bb, 1, 3)
                split_store(Aq, bA, 1, 6)
                split_store(dq, bd, 1, 9)
                fw.barrier()

        def phase_d2(l):
            with ExitStack() as st:
                nmax = max(np_, ns)
                kt = sbt(st, "d2k", [128, nmax // 128, 256], BF16); Bk = fw.buf("d2k", dma=True)
                vt = sbt(st, "d2v", [128, nmax // 128, 257], BF16); Bv = fw.buf("d2v", dma=True)
                R3 = Ring(st, "d2r", 2, [3, nmax], BF16, dma=True)
                T3 = Ring(st, "d2t", 2, [3, nmax], BF16, dma=True)
                acol = Ring(st, "d2a", 3, [128, 1], F32)
                ka = Ring(st, "d2ka", 3, [128, 256], BF16)
                Et = Ring(st, "d2E", 2, [128, 2, 258], F32, dma=True)
                pcl = Ring(st, "d2pc", 2, [128, 8], F32, psum=True)
                pE = Ring(st, "d2pE", 4, [128, 512], F32, psum=True)
                for si, (s0, nseg) in enumerate(segs):
                    nblk = nseg // 128
                    for h in range(4):
                        fw.dma("sp", kt[:, 0:nblk, :], ktok[s0:s0 + nseg, h * 256:(h + 1) * 256].rearrange("(b p) d -> p b d", p=128), Bk, writes=[Bk])
                        fw.op("dve", lambda e: e.memset(vt[:], 1.0), writes=[Bv])
                        fw.dma("sp", vt[:, 0:nblk, 0:256], vtok[s0:s0 + nseg, h * 256:(h + 1) * 256].rearrange("(b p) d -> p b d", p=128), Bv, writes=[Bv])
                        for dirn in range(2):
                            ch = si * 8 + dirn * 4 + h
                            r3, br3 = R3.next()
                            fw.dma("sp", r3[:, 0:nseg], drow[dirn * 4 + h, 6:9, s0:s0 + nseg], br3, writes=[br3])
                            t3, bt3 = T3.next()
                            rT = 3 if dirn == 0 else 9
                            fw.dma("sp", t3[:, 0:nseg], drow[dirn * 4 + h, rT:rT + 3, s0:s0 + nseg], bt3, writes=[bt3])
                            E0, bE0 = pE.next()
                            E1, bE1 = pE.next()
                            for b in range(nblk):
                                pc1, bpc1 = pcl.next()
                                fw.mm([lambda pe: pe.matmul(pc1[:, 0:1], lhsT=r3[0:3, b * 128:(b + 1) * 128], rhs=ones_bf[0:3, 0:1], start=True, stop=True)],
                                      reads=[br3, Bconst], writes=[bpc1])
                                a, ba = acol.next()
                                fw.op("act", lambda e: e.activation(out=a[:, 0:1], in_=pc1[:, 0:1], func=AF.Exp), reads=[bpc1], writes=[ba])
                                k2, bk2 = ka.next()
                                fw.op("dve", lambda e: e.tensor_scalar(out=k2[:, :], in0=kt[:, b, :], scalar1=a[:, 0:1], scalar2=None, op0=ALU.mult), reads=[Bk, ba], writes=[bk2])
                                fw.mm([lambda pe: pe.matmul(E0[:, 0:257], lhsT=k2[:, 0:128], rhs=vt[:, b, :], start=(b == 0), stop=(b == nblk - 1)),
                                       lambda pe: pe.matmul(E1[:, 0:257], lhsT=k2[:, 128:256], rhs=vt[:, b, :], start=(b == 0), stop=(b == nblk - 1))],
                                      reads=[bk2, Bv], writes=[bE0, bE1])
                            et, bet = Et.next()
                            fw.op("dve", lambda e: e.tensor_copy(out=et[:, 0, 0:257], in_=E0[:, 0:257]), reads=[bE0], writes=[bet])
                            fw.op("act", lambda e: e.copy(out=et[:, 1, 0:257], in_=E1[:, 0:257]), reads=[bE1], writes=[bet])
                            tcol = (nseg - 1) if dirn == 0 else 0
                            pc1, bpc1 = pcl.next()
                            fw.mm([lambda pe: pe.matmul(pc1[:, 0:1], lhsT=ones_bf[0:3, :], rhs=t3[0:3, tcol:tcol + 1], start=True, stop=True)], reads=[bt3, Bconst], writes=[bpc1])
                            fw.op("dve", lambda e: e.tensor_copy(out=et[:, 0, 257:258], in_=pc1[:, 0:1]), reads=[bpc1], writes=[bet])
                            fw.op("dve", lambda e: e.tensor_copy(out=et[:, 1, 257:258], in_=pc1[:, 0:1]), reads=[bpc1], writes=[bet])
                            fw.dma("pool", Erec[ch * 256:(ch + 1) * 256, :].rearrange("(k p) e -> p k e", p=128), et[:], bet, reads=[bet])
                fw.barrier()

        def phase_d34(l):
            with ExitStack() as st:
                nmax = max(np_, ns)
                Cin = sbt(st, "Cin", [128, 16, 2, 256], BF16)
                nrep = sbt(st, "nrep", [128, 16, 2, 128], BF16)
                BC = fw.buf("Cin")
                onesf = sbt(st, "onesf", [128, 128], F32)
                selm = sbt(st, "selm", [128, 16], F32)
                msk = sbt(st, "msk", [128, 2, 4, 512], BF16)
                Bsm = fw.buf("selm", dma=True)
                fw.dma("sp", selm[:], selm_in[:, :], Bsm, writes=[Bsm])
                fw.dma("sp", msk[:].rearrange("p a r t -> p (a r t)"), mask_in[:, :], Bsm, writes=[Bsm])
                fw.op("dve", lambda e: e.memset(onesf[:], 1.0), writes=[Bsm])
                with ExitStack() as s3:
                    Ea = Ring(s3, "Ea", 2, [128, NCORES, 2, 258], F32, dma=True)
                    S = sbt(s3, "S", [128, 2, 257], F32); BS = fw.buf("S")
                    tm = sbt(s3, "tm", [128, 2, 257], F32); Btm = fw.buf("tm")
                    sc = Ring(s3, "sc", 4, [128, 2], F32)
                    for ch in range(16):
                        dirn = (ch // 4) % 2
                        ea, bea = Ea.next()
                        for c in range(NCORES):
                            fw.dma("sp", ea[:, c, :, :], Eall[c * 4096 + ch * 256:c * 4096 + (ch + 1) * 256, :].rearrange("(k p) e -> p k e", p=128), bea, writes=[bea])
                        fw.op("dve", lambda e: e.memset(S[:], 0.0), writes=[BS])
                        order = range(NCORES) if dirn == 0 else range(NCORES - 1, -1, -1)
                        for c in order:
                            mcol = selm[:, dirn * 8 + c:dirn * 8 + c + 1]
                            d1, bd1 = sc.next()
                            fw.op("act", lambda e: e.activation(out=d1[:, 0:1], in_=ea[:, c, 0, 257:258], func=AF.Exp), reads=[bea], writes=[bd1])
                            fw.op("dve", lambda e: e.tensor_scalar(out=d1[:, 1:2], in0=d1[:, 0:1], scalar1=-1.0, scalar2=mcol, op0=ALU.add, op1=ALU.mult), reads=[bd1, Bsm], writes=[bd1])
                            fw.op("dve", lambda e: e.tensor_scalar(out=d1[:, 1:2], in0=d1[:, 1:2], scalar1=1.0, scalar2=None, op0=ALU.add), reads=[bd1], writes=[bd1])
                            fw.op("dve", lambda e: e.tensor_scalar(out=tm[:], in0=ea[:, c, :, 0:257], scalar1=mcol, scalar2=None, op0=ALU.mult), reads=[bea, Bsm], writes=[Btm])
                            fw.op("dve", lambda e: e.scalar_tensor_tensor(out=S[:], in0=S[:], scalar=d1[:, 1:2], in1=tm[:], op0=ALU.mult, op1=ALU.add), reads=[BS, bd1, Btm], writes=[BS])
                        fw.op("dve", lambda e: e.tensor_copy(out=Cin[:, ch, :, :], in_=S[:, :, 0:256]), reads=[BS], writes=[BC])
                        for dc in range(2):
                            fw.op("dve", lambda e: e.tensor_scalar(out=nrep[:, ch, dc, :], in0=onesf[:, :], scalar1=S[:, dc, 256:257], scalar2=None, op0=ALU.mult),
                                  reads=[BS, Bsm], writes=[BC])
                    fw.barrier()
                with ExitStack() as s4:
                    qh = sbt(s4, "qh", [128, 2, nmax], BF16); Bqh = fw.buf("qh", dma=True)
                    kh = sbt(s4, "kh", [128, 2, nmax], BF16); Bkh = fw.buf("kh", dma=True)
                    vh = sbt(s4, "vh", [128, nmax // 128, 256], BF16); Bvh = fw.buf("vh", dma=True)
                    L6 = Ring(s4, "L6", 2, [6, nmax], BF16, dma=True)
                    R6 = Ring(s4, "R6", 2, [6, nmax], BF16, dma=True)
                    D3r = Ring(s4, "D3r", 2, [3, nmax], BF16, dma=True)
                    qd = Ring(s4, "qd", 2, [128, 2, 512], BF16)
                    f32 = Ring(s4, "m32", 6, [128, 512], F32, dma=True)
                    W = Ring(s4, "W", 3, [128, 512], BF16)
                    pA = Ring(s4, "pA", 2, [128, 512], F32, psum=True)
                    pD = Ring(s4, "pD", 2, [128, 512], F32, psum=True)
                    pO = [pst(s4, "pO%d" % i, [128, 512], F32) for i in range(3)]
                    BpO = fw.buf("pO")
                    for si, (s0, nseg) in enumerate(segs):
                        nblk = nseg // 128
                        qtiles = [(t0, n) for (t0, n) in subs if s0 <= t0 < s0 + nseg]
                        for h in range(4):
                            fw.dma("sp", qh[:, :, 0:nseg], qT[h * 256:(h + 1) * 256, s0:s0 + nseg].rearrange("(k p) t -> p k t", p=128), Bqh, writes=[Bqh])
                            fw.dma("sp", kh[:, :, 0:nseg], kT[h * 256:(h + 1) * 256, s0:s0 + nseg].rearrange("(k p) t -> p k t", p=128), Bkh, writes=[Bkh])
                            fw.dma("sp", vh[:, 0:nblk, :], vtok[s0:s0 + nseg, h * 256:(h + 1) * 256].rearrange("(b p) d -> p b d", p=128), Bvh, writes=[Bvh])
                            for dirn in range(2):
                                ch = si * 8 + dirn * 4 + h
                                l6, bl6 = L6.next(); r6, br6 = R6.next(); d3, bd3 = D3r.next()
                                fw.op("dve", lambda e: e.memset(l6[:], 1.0), writes=[bl6])
                                fw.op("dve", lambda e: e.memset(r6[:], 1.0), writes=[br6])
                                fw.dma("sp", l6[0:3, 0:nseg], drow[dirn * 4 + h, 0:3, s0:s0 + nseg], bl6, writes=[bl6])
                                fw.dma("sp", r6[3:6, 0:nseg], drow[dirn * 4 + h, 3:6, s0:s0 + nseg], br6, writes=[br6])
                                fw.dma("sp", d3[:, 0:nseg], drow[dirn * 4 + h, 9:12, s0:s0 + nseg], bd3, writes=[bd3])
                                for (t0, n) in qtiles:
                                    lo_ = t0 - s0
                                    qb0, nqb = lo_ // 128, n // 128
                                    pd, bpd = pD.next()
                                    fw.mm([lambda pe: pe.matmul(pd[:, 0:n], lhsT=ones_bf[0:3, :], rhs=d3[0:3, lo_:lo_ + n], start=True, stop=True)], reads=[bd3, Bconst], writes=[bpd])
                                    ed, bed = f32.next()
                                    fw.op("act", lambda e: e.activation(out=ed[:, 0:n], in_=pd[:, 0:n], func=AF.Exp), reads=[bpd], writes=[bed])
                                    q2, bq2 = qd.next()
                                    for dc in range(2):
                                        fw.op("dve", lambda e, dc=dc: e.tensor_tensor(out=q2[:, dc, 0:n], in0=qh[:, dc, lo_:lo_ + n], in1=ed[:, 0:n], op=ALU.mult), reads=[Bqh, bed], writes=[bq2])
                                    kbs = list(range(0, qb0 + nqb)) if dirn == 0 else list(range(qb0, nblk))
                                    fw.mm([lambda pe, ec=ec, dc=dc: pe.matmul(pO[ec][:, 0:n], lhsT=Cin[:, ch, dc, ec * 128:(ec + 1) * 128], rhs=q2[:, dc, 0:n], start=(dc == 0), stop=False)
                                           for ec in range(2) for dc in range(2)] +
                                          [lambda pe, dc=dc: pe.matmul(pO[2][:, 0:n], lhsT=nrep[:, ch, dc, :], rhs=q2[:, dc, 0:n], start=(dc == 0), stop=False) for dc in range(2)],
                                          reads=[BC, bq2], writes=[BpO])
                                    def issue_AD(kb):
                                        a_, ba_ = pA.next()
                                        fw.mm([lambda pe, dc=dc: pe.matmul(a_[:, 0:n], lhsT=kh[:, dc, kb * 128:(kb + 1) * 128], rhs=qh[:, dc, lo_:lo_ + n], start=(dc == 0), stop=(dc == 1))
                                               for dc in range(2)], reads=[Bkh, Bqh], writes=[ba_])
                                        d_, bd_ = pD.next()
                                        fw.mm([lambda pe: pe.matmul(d_[:, 0:n], lhsT=l6[0:6, kb * 128:(kb + 1) * 128], rhs=r6[0:6, lo_:lo_ + n], start=True, stop=True)],
                                              reads=[bl6, br6], writes=[bd_])
                                        return a_, ba_, d_, bd_

                                    pend = [issue_AD(kbs[0])]
                                    for ki, kb in enumerate(kbs):
                                        last = (ki == len(kbs) - 1)
                                        a_, ba_, d_, bd_ = pend.pop(0)
                                        if not last:
                                            pend.append(issue_AD(kbs[ki + 1]))
                                        diag = qb0 <= kb < qb0 + nqb
                                        e_, be_ = f32.next()
                                        if diag:
                                            c_, bc_ = f32.next()
                                            fw.op("dve", lambda e: e.tensor_scalar(out=c_[:, 0:n], in0=d_[:, 0:n], scalar1=40.0, scalar2=None, op0=ALU.min), reads=[bd_], writes=[bc_])
                                            fw.op("act", lambda e: e.activation(out=e_[:, 0:n], in_=c_[:, 0:n], func=AF.Exp), reads=[bc_], writes=[be_])
                                        else:
                                            fw.op("act", lambda e: e.activation(out=e_[:, 0:n], in_=d_[:, 0:n], func=AF.Exp), reads=[bd_], writes=[be_])
                                        w_, bw_ = W.next()
                                        if diag:
                                            fw.op("dve", lambda e: e.tensor_tensor(out=e_[:, 0:n], in0=e_[:, 0:n], in1=msk[:, dirn, kb - qb0, 0:n], op=ALU.mult), reads=[be_, Bsm], writes=[be_])
                                        fw.op("dve", lambda e: e.tensor_tensor(out=w_[:, 0:n], in0=a_[:, 0:n], in1=e_[:, 0:n], op=ALU.mult), reads=[ba_, be_], writes=[bw_])
                                        fw.mm([lambda pe, ec=ec: pe.matmul(pO[ec][:, 0:n], lhsT=vh[:, kb, ec * 128:(ec + 1) * 128], rhs=w_[:, 0:n], start=False, stop=last) for ec in range(2)] +
                                              [lambda pe: pe.matmul(pO[2][:, 0:n], lhsT=ones_bf[:, :], rhs=w_[:, 0:n], start=False, stop=last)],
                                              reads=[Bvh, bw_, Bconst], writes=[BpO])
                                    dn, bdn = f32.next()
                                    fw.op("act", lambda e: e.activation(out=dn[:, 0:n], in_=pO[2][:, 0:n], func=AF.Abs), reads=[BpO], writes=[bdn])
                                    fw.op("dve", lambda e: e.tensor_scalar(out=dn[:, 0:n], in0=dn[:, 0:n], scalar1=1.0, scalar2=None, op0=ALU.max), reads=[bdn], writes=[bdn])
                                    fw.op("dve", lambda e: e.reciprocal(out=dn[:, 0:n], in_=dn[:, 0:n]), reads=[bdn], writes=[bdn])
                                    for ec in range(2):
                                        o_, bo_ = f32.next()
                                        fw.op("dve", lambda e, ec=ec: e.tensor_tensor(out=o_[:, 0:n], in0=pO[ec][:, 0:n], in1=dn[:, 0:n], op=ALU.mult), reads=[BpO, bdn], writes=[bo_])
                                        fw.dma("pool", hbuf[dirn, h * 256 + ec * 128:h * 256 + (ec + 1) * 128, t0:t0 + n], o_[:, 0:n], bo_, reads=[bo_])
                    fw.barrier()

        def phase_d5(l):
            with ExitStack() as st:
                hf = Ring(st, "hf", 2, [128, 2, 512], F32, dma=True)
                hb_ = Ring(st, "hb", 2, [128, 2, 512], F32, dma=True)
                go = Ring(st, "go", 2, [128, 2, 512], F32, dma=True)
                gm = Ring(st, "gm", 2, [128, 2, 512], F32, dma=True)
                sq = Ring(st, "d5sq", 2, [128, 2, 512], BF16)
                f32 = Ring(st, "d5f", 4, [128, 512], F32)
                ob = Ring(st, "d5o", 3, [128, 512], BF16, dma=True)
                pp = Ring(st, "d5p", 2, [128, 512], F32, psum=True)
                for (t0, n) in subs:
                    for h in range(4):
                        a, ba = hf.next(); b_, bb = hb_.next(); g1, bg1 = go.next(); g2, bg2 = gm.next()
                        rows = slice(1024 * 0 + h * 256, h * 256 + 256)
                        fw.dma("sp", a[:, :, 0:n], hbuf[0, h * 256:(h + 1) * 256, t0:t0 + n].rearrange("(k p) t -> p k t", p=128), ba, writes=[ba])
                        fw.dma("sp", b_[:, :, 0:n], hbuf[1, h * 256:(h + 1) * 256, t0:t0 + n].rearrange("(k p) t -> p k t", p=128), bb, writes=[bb])
                        fw.dma("sp", g1[:, :, 0:n], goT[h * 256:(h + 1) * 256, t0:t0 + n].rearrange("(k p) t -> p k t", p=128), bg1, writes=[bg1])
                        fw.dma("sp", g2[:, :, 0:n], gmzT[h * 256:(h + 1) * 256, t0:t0 + n].rearrange("(k p) t -> p k t", p=128), bg2, writes=[bg2])
                        fw.op("dve", lambda e: e.tensor_tensor(out=a[:, :, 0:n], in0=a[:, :, 0:n], in1=b_[:, :, 0:n], op=ALU.add), reads=[ba, bb], writes=[ba])
                        s_, bs_ = sq.next()
                        fw.op("dve", lambda e: e.tensor_tensor(out=s_[:, :, 0:n], in0=a[:, :, 0:n], in1=a[:, :, 0:n], op=ALU.mult), reads=[ba], writes=[bs_])
                        p_, bp_ = pp.next()
                        fw.mm([lambda pe, ec=ec: pe.matmul(p_[:, 0:n], lhsT=ones_bf[:, :], rhs=s_[:, ec, 0:n], start=(ec == 0), stop=(ec == 1)) for ec in range(2)],
                              reads=[bs_, Bconst], writes=[bp_])
                        r_, br_ = rstd_from(f32, p_[:, 0:n], 128, n, 1.0 / 256, [bp_])
                        fw.op("dve", lambda e: e.tensor_tensor(out=g1[:, :, 0:n], in0=g1[:, :, 0:n], in1=g2[:, :, 0:n], op=ALU.mult), reads=[bg1, bg2], writes=[bg1])
                        for ec in range(2):
                            t_, bt_ = f32.next()
                            fw.op("dve", lambda e, ec=ec: e.scalar_tensor_tensor(out=t_[:, 0:n], in0=a[:, ec, 0:n], scalar=pc[:, l, 56 + h * 2 + ec:57 + h * 2 + ec], in1=r_[:, 0:n],
                                                                               op0=ALU.mult, op1=ALU.mult), reads=[ba, br_, Bconst], writes=[bt_])
                            o_, bo_ = ob.next()
                            fw.op("dve", lambda e, ec=ec: e.tensor_tensor(out=o_[:, 0:n], in0=t_[:, 0:n], in1=g1[:, ec, 0:n], op=ALU.mult), reads=[bt_, bg1], writes=[bo_])
                            fw.dma("pool", mixT[1024 + h * 256 + ec * 128:1024 + h * 256 + (ec + 1) * 128, t0:t0 + n], o_[:, 0:n], bo_, reads=[bo_])
                fw.barrier()

        def zero_mix(r0, r1):
            with ExitStack() as st:
                z = sbt(st, "zt", [128, NT], BF16)
                Bz = fw.buf("z", dma=True)
                fw.op("dve", lambda e: e.memset(z[:], 0.0), writes=[Bz])
                for r in range(r0, r1, 128):
                    fw.dma("pool", mixT[r:r + 128, :], z[:], Bz, reads=[Bz])
                fw.barrier()

        for l in range(DEPTH):
            prep_weights(l)
        for l in range(DEPTH):
            xsrc = x_in if l == 0 else x1
            ydst = x1 if l == 0 else y_out
            phase_a(l, xsrc)
            phase_b(l)
            fw.collective(cc_key, lambda e: e.collective_compute("AllGather", ALU.bypass, replica_groups=[list(range(NCORES))],
                                                                 ins=[kvrec.opt()], outs=[kvall.opt()]))
            phase_e_edges(l)
            fw.collective(cc_key, lambda e: e.collective_compute("AllGather", ALU.bypass, replica_groups=[list(range(NCORES))],
                                                                 ins=[edrec.opt()], outs=[edall.opt()]))
            phase_d0(l)
            phase_d1(l)
            phase_d2(l)
            fw.collective(cc_key, lambda e: e.collective_compute("AllGather", ALU.bypass, replica_groups=[list(range(NCORES))],
                                                                 ins=[Erec.opt()], outs=[Eall.opt()]))
            fw.barrier()
            phase_e(l)
            phase_d34(l)
            phase_d5(l)
            phase_c(l)
            phase_f(l, xsrc, ydst)
        fw.barrier()
        print("instructions:", fw.ninst)
    return nc


def _mk_win_blocks():
    blocks = []
    specs = []

    def addF(src0, ncols, dname, post, scale=1.0):
        for j in range(ncols // 512):
            blocks.append(([(src0 + j * 512, 512, 0)], scale))
            specs.append(("F", [(q * 128, 128, dname, j * 512 + q * 128, post) for q in range(4)]))

    def addT(src0, ncols, dname, scale=1.0):
        for j in range(ncols // 512):
            blocks.append(([(src0 + j * 512, 512, 0)], scale))
            specs.append(("T", dname, j * 512))

    addF(0, 1024, "pxT", "copy")
    addF(1024, 1024, "gpzT", "silu")
    addF(2048, 1024, "qT", "copy")
    addF(3072, 1024, "kT", "copy", 1.0 / 16)
    addT(3072, 1024, "ktok", 1.0 / 16)
    addT(4096, 1024, "vtok")
    addF(5120, 1024, "goT", "sigmoid")
    addF(6144, 1024, "gmzT", "silu")
    addF(7184, 1536, "qlatT", "copy")
    addF(8720, 512, "kvlatT", "copy")
    addF(9296, 2048, "gazT", "silu")
    blocks.append(([(7168, 16, 0), (9232, 64, 128), (9264, 32, 256), (9232, 32, 288)], 1.0))
    specs.append(("S", [(0, 16, "gT", 0, "copy"), (128, 64, "krT", 0, "copy"), (256, 64, "krsT", 0, "copy")]))
    return blocks, specs


WIN_BLOCKS, A_SPECS = _mk_win_blocks()


def _mk_dmask():
    sidx = np.arange(128)[:, None, None]
    r = np.arange(4)[None, :, None]
    t = np.arange(512)[None, None, :]
    fwd = (t >= r * 128 + sidx)
    bwd = (t <= r * 128 + sidx)
    mk = np.stack([fwd, bwd], 1).astype(np.float32)
    return np.ascontiguousarray(mk.reshape(128, 2 * 4 * 512)).astype(ml_dtypes.bfloat16)


DMASK = _mk_dmask()
assert len(WIN_BLOCKS) == 25


def _pcols(norm_g, qlat_g, kvlat_g, pool_scale, mlstm_norm_g, qn_g, qr_g, kn_g, kr_g, gate_bias):
    pc = np.zeros((DEPTH, 128, NPC), np.float32)
    for l in range(DEPTH):
        pc[l, :, 0:32] = norm_g[l].reshape(32, 128).T
        pc[l, :, 32:44] = qlat_g[l].reshape(12, 128).T
        pc[l, :, 44:48] = kvlat_g[l].reshape(4, 128).T
        pc[l, :, 48:56] = pool_scale[l].reshape(8, 128).T
        pc[l, :, 56:64] = mlstm_norm_g[l].reshape(8, 128).T
        pc[l, :, 64] = qn_g[l]
        pc[l, :, 65] = kn_g[l]
        pc[l, 0:64, 66] = qr_g[l]
        pc[l, 0:64, 67] = np.concatenate([qr_g[l][32:], qr_g[l][:32]])
        pc[l, 0:64, 68] = kr_g[l]
        pc[l, 0:64, 69] = np.concatenate([kr_g[l][32:], kr_g[l][:32]])
        pc[l, 0:16, 70] = gate_bias[l]
    return pc


def _rope_tables(pos):
    inv_freq = (10000.0 ** (-(np.arange(0, 64, 2, dtype=np.float32) / 64.0))).astype(np.float32)
    ang = pos.astype(np.float32)[None, :] * inv_freq[:, None]
    c = np.cos(ang).astype(np.float32)
    s = np.sin(ang).astype(np.float32)
    return np.concatenate([c, c], 0), np.concatenate([-s, s], 0)


def run(inputs, sp, ss, dbg=()):
    np_, ns = sp // NCORES, ss // NCORES
    nc = build(np_, ns, dbg)
    f = lambda a: np.ascontiguousarray(np.asarray(a, dtype=np.float32))
    xp = f(inputs["x_prompt"])[0]
    xs = f(inputs["x_sample"])[0]
    pc = _pcols(*[f(inputs[k]) for k in ("norm_g", "qlat_g", "kvlat_g", "pool_scale", "mlstm_norm_g", "qn_g", "qr_g", "kn_g", "kr_g", "gate_bias")])
    ident = np.eye(128, dtype=np.float32).astype(ml_dtypes.bfloat16)
    shared = {"w_in": f(inputs["w_in"]), "w_out": f(inputs["w_out"]), "w_uq": f(inputs["w_uq"]), "w_ukv": f(inputs["w_ukv"]),
              "pool_w": f(inputs["pool_w"]), "pcols": pc, "ident": ident}
    in_maps = []
    for c in range(NCORES):
        pos = np.concatenate([np.arange(c * np_, (c + 1) * np_), np.arange(c * ns, (c + 1) * ns)])
        ct, stb = _rope_tables(pos)
        m = dict(shared)
        m["x"] = np.ascontiguousarray(np.concatenate([xp[c * np_:(c + 1) * np_], xs[c * ns:(c + 1) * ns]], 0))
        sel = np.zeros((128, 16), np.float32)
        if c > 0:
            sel[:, c - 1] = 1.0
        if c < NCORES - 1:
            sel[:, 8 + c + 1] = 1.0
        pcorr = np.ones((128, 4, 2, 16), np.float32)
        for g in range(4):
            w = 2 << g
            for si, (nloc, S) in enumerate(((np_, sp), (ns, ss))):
                for e in range(8):
                    t = c * nloc + e
                    cnt = min(t + w // 2, S) - max(t - w // 2, 0)
                    pcorr[:, g, si, e] = w / cnt
                    t = c * nloc + nloc - 8 + e
                    cnt = min(t + w // 2, S) - max(t - w // 2, 0)
                    pcorr[:, g, si, 8 + e] = w / cnt
        selm = np.zeros((128, 16), np.float32)
        selm[:, 0:c] = 1.0
        selm[:, 8 + c + 1:16] = 1.0
        m["selm"] = selm
        m["dmask"] = DMASK
        m["sel"] = sel
        m["pcorr"] = np.ascontiguousarray(pcorr.reshape(128, 128))
        m["cosT"] = np.ascontiguousarray(ct)
        m["sinT"] = np.ascontiguousarray(stb)
        in_maps.append(m)
    res = run_bass_kernel_spmd(nc, in_maps, core_ids=list(range(NCORES)))
    if dbg:
        return res.results
    ys = [np.asarray(r["y"]) for r in res.results]
    yp = np.concatenate([y[:np_] for y in ys], 0)[None]
    ysm = np.concatenate([y[np_:] for y in ys], 0)[None]
    return yp.astype(np.float32), ysm.astype(np.float32)


def kernel(**inputs):
    sp = inputs["x_prompt"].shape[1]
    ss = inputs["x_sample"].shape[1]
    return run(inputs, sp, ss)
```

```python
import math
from contextlib import ExitStack
import numpy as np
import ml_dtypes
import concourse.bass as bass
import concourse.mybir as mybir
from concourse.bass_utils import run_bass_kernel_spmd

F32 = mybir.dt.float32
BF16 = mybir.dt.bfloat16
AF = mybir.ActivationFunctionType
ALU = mybir.AluOpType

NCORES = 8
D = 4096
KC = 32
DEPTH = 2
NIN = 11344
EPS = 1e-6
NPC = 72


class Buf:
    __slots__ = ("name", "writers", "readers", "dsem")

    def __init__(self, name):
        self.name = name
        self.writers = {}
        self.readers = {}
        self.dsem = None


class FW:
    def __init__(self, nc, stack):
        self.nc = nc
        self.stack = stack
        self.eng = {"pe": nc.tensor, "act": nc.scalar, "dve": nc.vector, "pool": nc.gpsimd, "sp": nc.sync}
        self.ecount = {k: 0 for k in self.eng}
        self.sems = {}
        for k in self.eng:
            self.sems[("e", k)] = stack.enter_context(nc.semaphore("es_" + k))
        self.issued = {}
        self.waited = {k: {} for k in self.eng}
        self.ninst = 0
        self.nbuf = 0
        self.cc_keys = set()

    def new_dsem(self, name):
        key = ("d", name)
        self.sems[key] = self.stack.enter_context(self.nc.semaphore("ds_" + name))
        self.issued[key] = 0
        return key

    def buf(self, name, dma=False):
        self.nbuf += 1
        b = Buf(name)
        if dma:
            if not hasattr(self, "pool_keys"):
                self.pool_keys = [self.new_dsem("p%d" % i) for i in range(40)]
                self.pool_i = 0
            b.dsem = self.pool_keys[self.pool_i % len(self.pool_keys)]
            self.pool_i += 1
        return b

    def _wait(self, ek, toks):
        need = {}
        for (key, val) in toks:
            if key[0] == "d":
                val = self.issued[key]
            if key == ("e", "pe") and ek == "pe":
                continue
            if val > need.get(key, 0):
                need[key] = val
        w = self.waited[ek]
        for key, val in need.items():
            if w.get(key, 0) >= val:
                continue
            self.eng[ek].wait_ge(self.sems[key], val)
            w[key] = val

    def _deps(self, reads, writes):
        toks = []
        for b in reads:
            toks += list(b.writers.items())
        for b in writes:
            toks += list(b.writers.items())
            toks += list(b.readers.items())
        return toks

    def _commit(self, tok, reads, writes):
        k, v = tok
        for b in reads:
            if b.readers.get(k, 0) < v:
                b.readers[k] = v
        for b in writes:
            b.writers = {k: v}
            b.readers = {}

    def op(self, ek, fn, reads=(), writes=()):
        self._wait(ek, self._deps(reads, writes))
        ins = fn(self.eng[ek])
        self.ecount[ek] += 1
        ins.then_inc(self.sems[("e", ek)], 1)
        tok = (("e", ek), self.ecount[ek])
        self._commit(tok, reads, writes)
        self.ninst += 1
        return tok

    def mm(self, fns, reads=(), writes=()):
        self._wait("pe", self._deps(reads, writes))
        pe = self.eng["pe"]
        ins = None
        for fn in fns:
            ins = fn(pe)
        self.ecount["pe"] += 1
        ins.then_inc(self.sems[("e", "pe")], 1)
        tok = (("e", "pe"), self.ecount["pe"])
        self._commit(tok, reads, writes)
        self.ninst += len(fns)
        return tok

    def dma(self, qk, out_ap, in_ap, side, reads=(), writes=(), **kw):
        self._wait(qk, self._deps(reads, writes))
        key = side.dsem
        ins = self.eng[qk].dma_start(out=out_ap, in_=in_ap, **kw)
        ins.then_inc(self.sems[key], 16)
        self.issued[key] += 16
        tok = (key, self.issued[key])
        self._commit(tok, reads, writes)
        self.ninst += 1
        return tok

    def collective(self, key, fn):
        self.cc_keys.add(key)
        self.drain("pool")
        self._wait("pool", [(k, self.issued[k]) for k in self.cc_keys if self.issued[k] > 0])
        ins = fn(self.eng["pool"])
        ins.then_inc(self.sems[key], 1)
        self.issued[key] += 1

    def wait_key(self, key):
        for ek in self.eng:
            self._wait(ek, [(key, self.issued[key])])

    def drain(self, ek, skip=()):
        toks = [(("e", k), self.ecount[k]) for k in self.eng if self.ecount[k] > 0 and k != ek]
        toks += [(k, v) for k, v in self.issued.items() if v > 0 and k not in skip and k not in self.cc_keys]
        self._wait(ek, toks)

    def barrier(self, skip=()):
        for ek in self.eng:
            self.drain(ek, skip)
        self.pool_i = 0


def subtiles_of(np_, ns):
    out = []
    for (s0, n) in ((0, np_), (np_, ns)):
        t = 0
        while t < n:
            w = min(512, n - t)
            out.append((s0 + t, w))
            t += w
    return out


def build(np_, ns, dbg=()):
    NT = np_ + ns
    NB = NT // 128
    subs = subtiles_of(np_, ns)
    nc = bass.Bass("TRN2", target_bir_lowering=False)
    dt_in = lambda name, shape, dt=F32: nc.dram_tensor(name, shape, dt, kind="ExternalInput").ap()
    x_in = dt_in("x", [NT, D])
    w_in = dt_in("w_in", [DEPTH, D, NIN])
    w_out = dt_in("w_out", [DEPTH, D, D])
    w_uq = dt_in("w_uq", [DEPTH, 1536, 3072])
    w_ukv = dt_in("w_ukv", [DEPTH, 512, 4096])
    pool_w = dt_in("pool_w", [DEPTH, 4, 256, 256])
    pcols = dt_in("pcols", [DEPTH, 128, NPC])
    cosT = dt_in("cosT", [64, NT])
    sinT = dt_in("sinT", [64, NT])
    ident_in = dt_in("ident", [128, 128], BF16)
    sel_in = dt_in("sel", [128, 16])
    pcorr_in = dt_in("pcorr", [128, 128])
    selm_in = dt_in("selm", [128, 16])
    mask_in = dt_in("dmask", [128, 2 * 4 * 512], BF16)
    y_out = nc.dram_tensor("y", [NT, D], F32, kind="ExternalOutput").ap()

    scr = lambda name, shape, dt=BF16: (nc.dram_tensor(name, shape, dt, kind="ExternalOutput").ap() if name in dbg else nc.dram_tensor(name, shape, dt).ap())
    Win = scr("Win_bf", [DEPTH, 25, 128, KC * 512])
    Wout = scr("Wout_bf", [DEPTH, 8, 128, KC * 512])
    Wuq = scr("Wuq_bf", [DEPTH, 8, 128, 12 * 512])
    Wukv = scr("Wukv_bf", [DEPTH, 8, 128, 4 * 512])
    pxT = scr("pxT", [1024, NT], F32)
    gpzT = scr("gpzT", [1024, NT], F32)
    qT = scr("qT", [1024, NT])
    kT = scr("kT", [1024, NT])
    ktok = scr("ktok", [NT, 1024])
    vtok = scr("vtok", [NT, 1024])
    goT = scr("goT", [1024, NT], F32)
    gmzT = scr("gmzT", [1024, NT], F32)
    qlatT = scr("qlatT", [1536, NT])
    kvlatT = scr("kvlatT", [512, NT])
    gazT = scr("gazT", [2048, NT], F32)
    gT = scr("gT", [16, NT], F32)
    krT = scr("krT", [64, NT], F32)
    krsT = scr("krsT", [64, NT], F32)
    QT = scr("QT", [16 * 192, NT])
    KVR = 4160
    kvrec = scr("kvrec", [KVR, NT])
    kvall = scr("kvall", [NCORES * KVR, NT])
    mixT = scr("mixT", [D, NT])
    x1 = scr("x1", [NT, D], F32)
    grow = scr("grow", [3, 16, NT], F32)
    drow = scr("drow", [8, 12, NT])
    Erec = scr("Erec", [16 * 256, 258], F32)
    Eall = scr("Eall", [NCORES * 16 * 256, 258], F32)
    hbuf = scr("hbuf", [2, 1024, NT], F32)
    edrec = scr("edrec", [1024, 32], F32)
    edall = scr("edall", [NCORES * 1024, 32], F32)

    with ExitStack() as top:
        fw = FW(nc, top)
        cc_ed = fw.new_dsem("cc_ed")
        cc_E = fw.new_dsem("cc_E")
        cc_kv = fw.new_dsem("cc_kv")

        import itertools
        uid = itertools.count()
        sbt = lambda st, name, shape, dt: st.enter_context(nc.sbuf_tensor("sb_" + name + "_%d" % next(uid), shape, dt))
        pst = lambda st, name, shape, dt: st.enter_context(nc.psum_tensor("ps_" + name + "_%d" % next(uid), shape, dt))
        ident = sbt(top, "ident", [128, 128], BF16)
        ones_bf = sbt(top, "ones_bf", [128, 128], BF16)
        pc = sbt(top, "pc", [128, DEPTH, NPC], F32)
        pcs = sbt(top, "pcs", [128, DEPTH, 8], F32)
        Bconst = fw.buf("const", dma=True)
        fw.dma("sp", ident[:], ident_in[:, :], Bconst, writes=[Bconst])
        for l in range(DEPTH):
            fw.dma("sp", pc[:, l, :], pcols[l, :, :], Bconst, writes=[Bconst])
        fw.op("dve", lambda e: e.memset(ones_bf[:], 1.0), writes=[Bconst])
        qscale = 192.0 ** -0.5
        for l in range(DEPTH):
            fw.op("dve", lambda e, l=l: e.tensor_scalar(out=pcs[:, l, 0:1], in0=pc[:, l, 64:65], scalar1=qscale, scalar2=None, op0=ALU.mult),
                  reads=[Bconst], writes=[Bconst])
            fw.op("dve", lambda e, l=l: e.tensor_scalar(out=pcs[:, l, 1:3], in0=pc[:, l, 66:68], scalar1=qscale, scalar2=None, op0=ALU.mult),
                  reads=[Bconst], writes=[Bconst])
        fw.barrier()

        def prep_weights(l):
            with ExitStack() as st:
                s32 = [sbt(st, "s32_%d" % i, [128, 8, 512], F32) for i in range(2)]
                s16 = [sbt(st, "s16_%d" % i, [128, KC, 512], BF16) for i in range(2)]
                B32 = [fw.buf("s32", dma=True) for _ in range(2)]
                B16 = [fw.buf("s16", dma=True) for _ in range(2)]
                for i in range(2):
                    fw.op("dve", lambda e, i=i: e.memset(s32[i][:], 0.0), writes=[B32[i]])
                cnt = {"ld": 0, "blk": 0, "cast": 0}

                def prep_block(src, nkc, pieces, gcol, scale, dst):
                    bi = cnt["blk"] % 2
                    cnt["blk"] += 1
                    for kg in range(0, nkc, 8):
                        nk = min(8, nkc - kg)
                        li = cnt["ld"] % 2
                        cnt["ld"] += 1
                        for (sc, w, do) in pieces:
                            fw.dma("sp", s32[li][:, 0:nk, do:do + w],
                                   src[kg * 128:(kg + nk) * 128, sc:sc + w].rearrange("(k p) c -> p k c", p=128),
                                   B32[li], writes=[B32[li]])
                        for k in range(nk):
                            ek = "dve" if cnt["cast"] % 2 == 0 else "pool"
                            cnt["cast"] += 1
                            if gcol is not None:
                                fn = lambda e, li=li, k=k, kk=kg + k: e.tensor_scalar(
                                    out=s16[bi][:, kk, :], in0=s32[li][:, k, :], scalar1=pc[:, l, gcol + kk:gcol + kk + 1],
                                    scalar2=float(scale), op0=ALU.mult, op1=ALU.mult)
                            else:
                                fn = lambda e, li=li, k=k, kk=kg + k: e.tensor_copy(out=s16[bi][:, kk, :], in_=s32[li][:, k, :])
                            fw.op(ek, fn, reads=[B32[li], Bconst], writes=[B16[bi]])
                    fw.dma("pool", dst, s16[bi][:, 0:nkc, :].rearrange("p k c -> p (k c)"), B16[bi], reads=[B16[bi]])

                for b, (pieces, scale) in enumerate(WIN_BLOCKS):
                    prep_block(w_in[l], KC, pieces, 0, scale, Win[l, b])
                for b in range(8):
                    prep_block(w_out[l], KC, [(b * 512, 512, 0)], None, 1.0, Wout[l, b])
                for b in range(8):
                    pieces = []
                    for hh in range(2):
                        h = 2 * b + hh
                        pieces += [(h * 192, 192, hh * 256), (h * 192 + 160, 32, hh * 256 + 192), (h * 192 + 128, 32, hh * 256 + 224)]
                    prep_block(w_uq[l], 12, pieces, 32, 1.0, Wuq[l, b])
                for b in range(4):
                    pieces = [((4 * b + hh) * 256, 128, hh * 128) for hh in range(4)]
                    prep_block(w_ukv[l], 4, pieces, 44, 1.0, Wukv[l, b])
                for b in range(4):
                    pieces = [((4 * b + hh) * 256 + 128, 128, hh * 128) for hh in range(4)]
                    prep_block(w_ukv[l], 4, pieces, 44, 1.0, Wukv[l, 4 + b])
                fw.barrier()

        def phase_a(l, xsrc):
            with ExitStack() as st:
                hT = sbt(st, "hT", [128, KC, 1024], BF16)
                wr = [sbt(st, "wr%d" % i, [128, KC, 512], BF16) for i in range(2)]
                xs = [sbt(st, "xs%d" % i, [128, D], F32) for i in range(2)]
                xn = sbt(st, "xn", [128, D], BF16)
                stat = sbt(st, "stat", [128, 4], F32)
                ost = [sbt(st, "ost%d" % i, [128, 512], F32) for i in range(4)]
                osb = [sbt(st, "osb%d" % i, [128, 512], BF16) for i in range(4)]
                pt = [pst(st, "pt%d" % i, [128, 8, 128], BF16) for i in range(2)]
                pa = [pst(st, "pa%d" % i, [128, 512], F32) for i in range(4)]
                BhT = fw.buf("hT")
                Bwr = [fw.buf("wr", dma=True) for _ in range(2)]
                Bxs = [fw.buf("xs", dma=True) for _ in range(2)]
                Bxn = fw.buf("xn")
                Bstat = fw.buf("stat")
                Bost = [fw.buf("ost", dma=True) for _ in range(4)]
                Bpt = [fw.buf("pt") for _ in range(2)]
                Bpa = [fw.buf("pa") for _ in range(4)]
                ctr = {"x": 0, "pt": 0, "w": 0, "pa": 0, "o": 0, "ev": 0}

                tiles = []
                cur = []
                for s in subs:
                    if cur and sum(n for _, n in cur) + s[1] > 1024:
                        tiles.append(cur)
                        cur = []
                    cur.append(s)
                if cur:
                    tiles.append(cur)

                for tile in tiles:
                    T0 = tile[0][0]
                    TN = sum(n for _, n in tile)
                    for bo in range(0, TN, 128):
                        xi = ctr["x"] % 2
                        ctr["x"] += 1
                        fw.dma("sp", xs[xi][:], xsrc[T0 + bo:T0 + bo + 128, :], Bxs[xi], writes=[Bxs[xi]])
                        fw.op("act", lambda e, xi=xi: e.activation(out=xn[:], in_=xs[xi][:], func=AF.Square, accum_out=stat[:, 0:1]),
                              reads=[Bxs[xi]], writes=[Bxn, Bstat])
                        fw.op("act", lambda e: e.activation(out=stat[:, 1:2], in_=stat[:, 0:1], func=AF.Ln, scale=1.0 / D, bias=EPS),
                              reads=[Bstat], writes=[Bstat])
                        fw.op("act", lambda e: e.activation(out=stat[:, 2:3], in_=stat[:, 1:2], func=AF.Exp, scale=-0.5),
                              reads=[Bstat], writes=[Bstat])
                        fw.op("dve", lambda e, xi=xi: e.tensor_scalar(out=xn[:], in0=xs[xi][:], scalar1=stat[:, 2:3], scalar2=None, op0=ALU.mult),
                              reads=[Bxs[xi], Bstat], writes=[Bxn])
                        for g4 in range(4):
                            pi = ctr["pt"] % 2
                            ctr["pt"] += 1
                            fw.mm([lambda pe, k=k, pi=pi, g4=g4: pe.transpose(out=pt[pi][:, k, :], in_=xn[:, (g4 * 8 + k) * 128:(g4 * 8 + k + 1) * 128], identity=ident[:])
                                   for k in range(8)], reads=[Bxn, Bconst], writes=[Bpt[pi]])
                            ek = "act" if g4 % 2 == 0 else "dve"
                            if ek == "act":
                                fn = lambda e, pi=pi, g4=g4, bo=bo: e.copy(out=hT[:, g4 * 8:(g4 + 1) * 8, bo:bo + 128], in_=pt[pi][:])
                            else:
                                fn = lambda e, pi=pi, g4=g4, bo=bo: e.tensor_copy(out=hT[:, g4 * 8:(g4 + 1) * 8, bo:bo + 128], in_=pt[pi][:])
                            fw.op(ek, fn, reads=[Bpt[pi]], writes=[BhT])

                    def evac(pi, M, n, post, dest_ap, use_bf):
                        oi = ctr["o"] % 4
                        ctr["o"] += 1
                        o = osb[oi] if use_bf else ost[oi]
                        if post == "silu":
                            fw.op("act", lambda e: e.activation(out=o[0:M, 0:n], in_=pa[pi][0:M, 0:n], func=AF.Silu), reads=[Bpa[pi]], writes=[Bost[oi]])
                        elif post == "sigmoid":
                            fw.op("act", lambda e: e.activation(out=o[0:M, 0:n], in_=pa[pi][0:M, 0:n], func=AF.Sigmoid), reads=[Bpa[pi]], writes=[Bost[oi]])
                        else:
                            ctr["ev"] += 1
                            if ctr["ev"] % 2 == 0:
                                fw.op("act", lambda e: e.copy(out=o[0:M, 0:n], in_=pa[pi][0:M, 0:n]), reads=[Bpa[pi]], writes=[Bost[oi]])
                            else:
                                fw.op("dve", lambda e: e.tensor_copy(out=o[0:M, 0:n], in_=pa[pi][0:M, 0:n]), reads=[Bpa[pi]], writes=[Bost[oi]])
                        fw.dma("pool", dest_ap, o[0:M, 0:n], Bost[oi], reads=[Bost[oi]])

                    for b, spec in enumerate(A_SPECS):
                        wi = ctr["w"] % 2
                        ctr["w"] += 1
                        fw.dma("sp", wr[wi][:].rearrange("p k c -> p (k c)"), Win[l, b], Bwr[wi], writes=[Bwr[wi]])
                        kind = spec[0]
                        if kind == "T":
                            dest = {"ktok": ktok, "vtok": vtok}[spec[1]]
                            c0 = spec[2]
                            for bo in range(0, TN, 128):
                                pi = ctr["pa"] % 4
                                ctr["pa"] += 1
                                fw.mm([lambda pe, kc=kc, pi=pi, bo=bo, wi=wi: pe.matmul(pa[pi][:, :], lhsT=hT[:, kc, bo:bo + 128], rhs=wr[wi][:, kc, :],
                                                                                         start=(kc == 0), stop=(kc == KC - 1)) for kc in range(KC)],
                                      reads=[BhT, Bwr[wi]], writes=[Bpa[pi]])
                                evac(pi, 128, 512, "copy", dest[T0 + bo:T0 + bo + 128, c0:c0 + 512], True)
                        else:
                            for (off, M, dname, r0, post) in spec[1]:
                                dest, use_bf = {"pxT": (pxT, False), "gpzT": (gpzT, False), "qT": (qT, True), "kT": (kT, True),
                                                "goT": (goT, False), "gmzT": (gmzT, False), "qlatT": (qlatT, True), "kvlatT": (kvlatT, True),
                                                "gazT": (gazT, False), "gT": (gT, False), "krT": (krT, False), "krsT": (krsT, False)}[dname]
                                so = 0
                                for (t0, n) in tile:
                                    pi = ctr["pa"] % 4
                                    ctr["pa"] += 1
                                    fw.mm([lambda pe, kc=kc, pi=pi, so=so, n=n, wi=wi, off=off, M=M: pe.matmul(
                                        pa[pi][0:M, 0:n], lhsT=wr[wi][:, kc, off:off + M], rhs=hT[:, kc, so:so + n],
                                        start=(kc == 0), stop=(kc == KC - 1)) for kc in range(KC)],
                                        reads=[BhT, Bwr[wi]], writes=[Bpa[pi]])
                                    evac(pi, M, n, post, dest[r0:r0 + M, t0:t0 + n], use_bf)
                                    so += n
                fw.barrier()

        def phase_f(l, xsrc, ydst):
            with ExitStack() as st:
                mt = sbt(st, "mt", [128, KC, 512], BF16)
                wr = [sbt(st, "fwr%d" % i, [128, KC, 512], BF16) for i in range(2)]
                xr = [sbt(st, "fxr%d" % i, [128, 512], F32) for i in range(4)]
                pa = [pst(st, "fpa%d" % i, [128, 512], F32) for i in range(4)]
                Bmt = fw.buf("mt", dma=True)
                Bwr = [fw.buf("fwr", dma=True) for _ in range(2)]
                Bxr = [fw.buf("fxr", dma=True) for _ in range(4)]
                Bpa = [fw.buf("fpa") for _ in range(4)]
                ctr = {"w": 0, "x": 0, "pa": 0}
                for (t0, n) in subs:
                    fw.dma("sp", mt[:, :, 0:n], mixT[:, t0:t0 + n].rearrange("(k p) t -> p k t", p=128), Bmt, writes=[Bmt])
                    for b in range(8):
                        wi = ctr["w"] % 2
                        ctr["w"] += 1
                        fw.dma("sp", wr[wi][:].rearrange("p k c -> p (k c)"), Wout[l, b], Bwr[wi], writes=[Bwr[wi]])
                        for bo in range(0, n, 128):
                            xi = ctr["x"] % 4
                            ctr["x"] += 1
                            pi = ctr["pa"] % 4
                            ctr["pa"] += 1
                            fw.dma("sp", xr[xi][:], xsrc[t0 + bo:t0 + bo + 128, b * 512:(b + 1) * 512], Bxr[xi], writes=[Bxr[xi]])
                            fw.mm([lambda pe, kc=kc, pi=pi, bo=bo, wi=wi: pe.matmul(pa[pi][:, :], lhsT=mt[:, kc, bo:bo + 128], rhs=wr[wi][:, kc, :],
                                                                                     start=(kc == 0), stop=(kc == KC - 1)) for kc in range(KC)],
                                  reads=[Bmt, Bwr[wi]], writes=[Bpa[pi]])
                            fw.op("dve", lambda e, xi=xi, pi=pi: e.tensor_tensor(out=xr[xi][:], in0=pa[pi][:], in1=xr[xi][:], op=ALU.add),
                                  reads=[Bpa[pi], Bxr[xi]], writes=[Bxr[xi]])
                            fw.dma("pool", ydst[t0 + bo:t0 + bo + 128, b * 512:(b + 1) * 512], xr[xi][:], Bxr[xi], reads=[Bxr[xi]])
                fw.barrier()


        class Ring:
            def __init__(self, st, name, n, shape, dt, psum=False, dma=False):
                mk = pst if psum else sbt
                self.t = [mk(st, "%s%d" % (name, i), shape, dt) for i in range(n)]
                self.b = [fw.buf(name, dma=dma) for _ in range(n)]
                self.i = 0

            def next(self):
                k = self.i % len(self.t)
                self.i += 1
                return self.t[k], self.b[k]

        def rstd_from(st_ring, src_ap, M, n, scale, reads):
            t1, b1 = st_ring.next()
            fw.op("act", lambda e: e.activation(out=t1[0:M, 0:n], in_=src_ap, func=AF.Ln, scale=scale, bias=EPS), reads=reads, writes=[b1])
            t2, b2 = st_ring.next()
            fw.op("act", lambda e: e.activation(out=t2[0:M, 0:n], in_=t1[0:M, 0:n], func=AF.Exp, scale=-0.5), reads=[b1], writes=[b2])
            return t2, b2

        def phase_b(l):
            with ExitStack() as st:
                ql = sbt(st, "ql", [128, 12, 512], BF16)
                kvl = sbt(st, "kvl", [128, 4, 512], BF16)
                sq = sbt(st, "sq", [128, 12, 512], BF16)
                Bql = fw.buf("ql", dma=True); Bkvl = fw.buf("kvl", dma=True); Bsq = fw.buf("sq")
                wq = Ring(st, "wq", 2, [128, 12, 512], BF16, dma=True)
                wk = Ring(st, "wk", 2, [128, 4, 512], BF16, dma=True)
                rope = sbt(st, "rope", [64, 4, 512], F32)
                Brope = fw.buf("rope", dma=True)
                keep = sbt(st, "keep", [128, 4, 512], F32)
                Bkeep = fw.buf("keep")
                rkvc = sbt(st, "rkvc", [128, 4], F32)
                Brkvc = fw.buf("rkvc")
                f32r = Ring(st, "bf32", 8, [128, 512], F32)
                b16r = Ring(st, "bb16", 6, [128, 512], BF16, dma=True)
                psr = Ring(st, "bps", 7, [128, 512], F32, psum=True)
                psc = pst(st, "bpsc", [128, 8], F32)
                Bpsc = fw.buf("psc")
                for (t0, n) in subs:
                    fw.dma("sp", ql[:, :, 0:n], qlatT[:, t0:t0 + n].rearrange("(k p) t -> p k t", p=128), Bql, writes=[Bql])
                    fw.dma("sp", kvl[:, :, 0:n], kvlatT[:, t0:t0 + n].rearrange("(k p) t -> p k t", p=128), Bkvl, writes=[Bkvl])
                    for i, src in enumerate((krT, krsT, cosT, sinT)):
                        fw.dma("sp", rope[:, i, 0:n], src[:, t0:t0 + n], Brope, writes=[Brope])
                    for (lat, Blat, nk, kbase, dim) in ((ql, Bql, 12, 0, 1536.0), (kvl, Bkvl, 4, 2, 512.0)):
                        fw.op("dve", lambda e, lat=lat, nk=nk: e.tensor_tensor(out=sq[:, 0:nk, 0:n], in0=lat[:, 0:nk, 0:n], in1=lat[:, 0:nk, 0:n], op=ALU.mult),
                              reads=[Blat], writes=[Bsq])
                        p, bp = psr.next()
                        fw.mm([lambda pe, j=j, p=p, nk=nk: pe.matmul(p[:, 0:n], lhsT=ones_bf[:, :], rhs=sq[:, j, 0:n], start=(j == 0), stop=(j == nk - 1))
                               for j in range(nk)], reads=[Bsq, Bconst], writes=[bp])
                        r, br = rstd_from(f32r, p[:, 0:n], 128, n, 1.0 / dim, [bp])
                        fw.op("dve", lambda e, r=r, kbase=kbase: e.tensor_copy(out=keep[:, kbase, 0:n], in_=r[:, 0:n]), reads=[br], writes=[Bkeep])
                        fw.op("dve", lambda e, r=r, kbase=kbase: e.tensor_tensor(out=keep[:, kbase + 1, 0:n], in0=r[:, 0:n], in1=r[:, 0:n], op=ALU.mult),
                              reads=[br], writes=[Bkeep])
                        if lat is kvl:
                            for bi in range(n // 128):
                                fw.mm([lambda pe, j=j, bi=bi: pe.matmul(psc[:, bi:bi + 1], lhsT=sq[:, j, bi * 128:(bi + 1) * 128], rhs=ones_bf[:, 0:1],
                                                                        start=(j == 0), stop=(j == 3)) for j in range(4)], reads=[Bsq, Bconst], writes=[Bpsc])
                            nb_ = n // 128
                            t1, b1 = f32r.next()
                            fw.op("act", lambda e, t1=t1: e.activation(out=t1[:, 0:nb_], in_=psc[:, 0:nb_], func=AF.Ln, scale=1.0 / 512, bias=EPS), reads=[Bpsc], writes=[b1])
                            fw.op("act", lambda e, t1=t1: e.activation(out=rkvc[:, 0:nb_], in_=t1[:, 0:nb_], func=AF.Exp, scale=-0.5), reads=[b1], writes=[Brkvc])
                    rq, rq2, rkv, rkv2 = keep[:, 0, :], keep[:, 1, :], keep[:, 2, :], keep[:, 3, :]

                    def normed(P, bP, M, rr, rr2, dim, gcol_ap, extra=None):
                        s16, bs16 = b16r.next()
                        fw.op("act", lambda e: e.activation(out=s16[0:M, 0:n], in_=P[0:M, 0:n], func=AF.Square), reads=[bP], writes=[bs16])
                        p4, bp4 = psr.next()
                        fw.mm([lambda pe: pe.matmul(p4[:, 0:n], lhsT=ones_bf[0:M, :], rhs=s16[0:M, 0:n], start=True, stop=True)], reads=[bs16, Bconst], writes=[bp4])
                        u, bu = f32r.next()
                        fw.op("dve", lambda e: e.tensor_tensor(out=u[0:M, 0:n], in0=p4[0:M, 0:n], in1=rr2[0:M, 0:n], op=ALU.mult), reads=[bp4, Bkeep], writes=[bu])
                        w, bw = rstd_from(f32r, u[0:M, 0:n], M, n, 1.0 / dim, [bu])
                        f, bf = f32r.next()
                        fw.op("dve", lambda e: e.tensor_tensor(out=f[0:M, 0:n], in0=w[0:M, 0:n], in1=rr[0:M, 0:n], op=ALU.mult), reads=[bw, Bkeep], writes=[bf])
                        return f, bf

                    def rope_combine(Pa, bPa, Pb, bPb, f, bf, ga, gb, dest_ap):
                        a, ba = f32r.next()
                        fw.op("dve", lambda e: e.scalar_tensor_tensor(out=a[0:64, 0:n], in0=Pa, scalar=ga, in1=f[0:64, 0:n], op0=ALU.mult, op1=ALU.mult),
                              reads=[bPa, bf, Bconst], writes=[ba])
                        b_, bb = f32r.next()
                        fw.op("dve", lambda e: e.scalar_tensor_tensor(out=b_[0:64, 0:n], in0=Pb, scalar=gb, in1=f[0:64, 0:n], op0=ALU.mult, op1=ALU.mult),
                              reads=[bPb, bf, Bconst], writes=[bb])
                        fw.op("dve", lambda e: e.tensor_tensor(out=a[0:64, 0:n], in0=a[0:64, 0:n], in1=rope[:, 2, 0:n], op=ALU.mult), reads=[ba, Brope], writes=[ba])
                        fw.op("dve", lambda e: e.tensor_tensor(out=b_[0:64, 0:n], in0=b_[0:64, 0:n], in1=rope[:, 3, 0:n], op=ALU.mult), reads=[bb, Brope], writes=[bb])
                        o, bo = b16r.next()
                        fw.op("dve", lambda e: e.tensor_tensor(out=o[0:64, 0:n], in0=a[0:64, 0:n], in1=b_[0:64, 0:n], op=ALU.add), reads=[ba, bb], writes=[bo])
                        fw.dma("pool", dest_ap, o[0:64, 0:n], bo, reads=[bo])

                    s16, bs16 = b16r.next()
                    fw.op("dve", lambda e: e.tensor_tensor(out=s16[0:64, 0:n], in0=rope[:, 0, 0:n], in1=rope[:, 0, 0:n], op=ALU.mult), reads=[Brope], writes=[bs16])
                    p4, bp4 = psr.next()
                    fw.mm([lambda pe: pe.matmul(p4[:, 0:n], lhsT=ones_bf[0:64, :], rhs=s16[0:64, 0:n], start=True, stop=True)], reads=[bs16, Bconst], writes=[bp4])
                    fk, bfk = rstd_from(f32r, p4[0:64, 0:n], 64, n, 1.0 / 64, [bp4])
                    rope_combine(rope[:, 0, 0:n], Brope, rope[:, 1, 0:n], Brope, fk, bfk, pc[0:64, l, 68:69], pc[0:64, l, 69:70], kvrec[2048:2112, t0:t0 + n])

                    for h in range(16):
                        if h % 2 == 0:
                            wqt, bwq = wq.next()
                            fw.dma("sp", wqt[:].rearrange("p k c -> p (k c)"), Wuq[l, h // 2], bwq, writes=[bwq])
                        if h % 4 == 0:
                            wkt, bwk = wk.next()
                            fw.dma("sp", wkt[:].rearrange("p k c -> p (k c)"), Wukv[l, h // 4], bwk, writes=[bwk])
                        c0 = (h % 2) * 256
                        P1, bP1 = psr.next()
                        fw.mm([lambda pe, j=j: pe.matmul(P1[:, 0:n], lhsT=wqt[:, j, c0:c0 + 128], rhs=ql[:, j, 0:n], start=(j == 0), stop=(j == 11)) for j in range(12)],
                              reads=[bwq, Bql], writes=[bP1])
                        P2, bP2 = psr.next()
                        fw.mm([lambda pe, j=j: pe.matmul(P2[0:64, 0:n], lhsT=wqt[:, j, c0 + 128:c0 + 192], rhs=ql[:, j, 0:n], start=(j == 0), stop=(j == 11)) for j in range(12)],
                              reads=[bwq, Bql], writes=[bP2])
                        P3, bP3 = psr.next()
                        fw.mm([lambda pe, j=j: pe.matmul(P3[0:64, 0:n], lhsT=wqt[:, j, c0 + 192:c0 + 256], rhs=ql[:, j, 0:n], start=(j == 0), stop=(j == 11)) for j in range(12)],
                              reads=[bwq, Bql], writes=[bP3])
                        f, bf = normed(P1, bP1, 128, rq, rq2, 128.0, None)
                        o, bo = b16r.next()
                        fw.op("dve", lambda e: e.scalar_tensor_tensor(out=o[:, 0:n], in0=P1[:, 0:n], scalar=pcs[:, l, 0:1], in1=f[:, 0:n], op0=ALU.mult, op1=ALU.mult),
                              reads=[bP1, bf, Bconst], writes=[bo])
                        fw.dma("pool", QT[h * 192:h * 192 + 128, t0:t0 + n], o[:, 0:n], bo, reads=[bo])
                        f2, bf2 = normed(P2, bP2, 64, rq, rq2, 64.0, None)
                        rope_combine(P2[0:64, 0:n], bP2, P3[0:64, 0:n], bP3, f2, bf2, pcs[0:64, l, 1:2], pcs[0:64, l, 2:3], QT[h * 192 + 128:h * 192 + 192, t0:t0 + n])
                        k0 = (h % 4) * 128
                        P5, bP5 = psr.next()
                        fw.mm([lambda pe, j=j: pe.matmul(P5[:, 0:n], lhsT=wkt[:, j, k0:k0 + 128], rhs=kvl[:, j, 0:n], start=(j == 0), stop=(j == 3)) for j in range(4)],
                              reads=[bwk, Bkvl], writes=[bP5])
                        f3, bf3 = normed(P5, bP5, 128, rkv, rkv2, 128.0, None)
                        o, bo = b16r.next()
                        fw.op("dve", lambda e: e.scalar_tensor_tensor(out=o[:, 0:n], in0=P5[:, 0:n], scalar=pc[:, l, 65:66], in1=f3[:, 0:n], op0=ALU.mult, op1=ALU.mult),
                              reads=[bP5, bf3, Bconst], writes=[bo])
                        fw.dma("pool", kvrec[h * 128:(h + 1) * 128, t0:t0 + n], o[:, 0:n], bo, reads=[bo])
                    for vb in range(4):
                        wkt, bwk = wk.next()
                        fw.dma("sp", wkt[:].rearrange("p k c -> p (k c)"), Wukv[l, 4 + vb], bwk, writes=[bwk])
                        for bi in range(n // 128):
                            P7, bP7 = psr.next()
                            fw.mm([lambda pe, j=j: pe.matmul(P7[:, :], lhsT=kvl[:, j, bi * 128:(bi + 1) * 128], rhs=wkt[:, j, :], start=(j == 0), stop=(j == 3)) for j in range(4)],
                                  reads=[bwk, Bkvl], writes=[bP7])
                            o, bo = b16r.next()
                            fw.op("dve", lambda e: e.tensor_scalar(out=o[:, :], in0=P7[:, :], scalar1=rkvc[:, bi:bi + 1], scalar2=None, op0=ALU.mult),
                                  reads=[bP7, Brkvc], writes=[bo])
                            tb = (t0 + bi * 128)
                            fw.dma("pool", kvrec[2112 + vb * 512:2112 + (vb + 1) * 512, tb:tb + 128].rearrange("(hh p) v -> p hh v", p=128),
                                   o[:, :].rearrange("p (hh v) -> p hh v", hh=4), bo, reads=[bo])
                fw.barrier()

        def phase_c(l):
            with ExitStack() as st:
                nmax = max(np_, ns)
                Kn = Ring(st, "Kn", 2, [128, NCORES, nmax], BF16, dma=True)
                Vh = Ring(st, "Vh", 2, [128, NCORES, nmax], BF16, dma=True)
                Kpe = sbt(st, "Kpe", [64, NCORES, nmax], BF16)
                BKpe = fw.buf("Kpe", dma=True)
                Qn = Ring(st, "Qn", 2, [128, 512], BF16, dma=True)
                Qp = Ring(st, "Qp", 2, [64, 512], BF16, dma=True)
                gz = Ring(st, "gz", 2, [128, 512], F32, dma=True)
                PT = Ring(st, "PT", 3, [128, 512], BF16)
                tmp = Ring(st, "ctmp", 4, [128, 512], F32)
                ob = Ring(st, "cob", 2, [128, 512], BF16, dma=True)
                STr = Ring(st, "ST", 4, [128, 512], F32, psum=True)
                Or = Ring(st, "O", 2, [128, 512], F32, psum=True)
                Lr = Ring(st, "L", 2, [128, 512], F32, psum=True)
                kva = kvall.rearrange("(c r) t -> r c t", c=NCORES)
                for (s0, nseg) in ((0, np_), (np_, ns)):
                    nblk = nseg // 128
                    fw.dma("sp", Kpe[:, :, 0:nseg], kva[2048:2112, :, s0:s0 + nseg], BKpe, writes=[BKpe])
                    qtiles = [(t0, n) for (t0, n) in subs if s0 <= t0 < s0 + nseg]
                    for h in range(16):
                        knt, bkn = Kn.next()
                        fw.dma("sp", knt[:, :, 0:nseg], kva[h * 128:(h + 1) * 128, :, s0:s0 + nseg], bkn, writes=[bkn])
                        vht, bvh = Vh.next()
                        fw.dma("sp", vht[:, :, 0:nseg], kva[2112 + h * 128:2112 + (h + 1) * 128, :, s0:s0 + nseg], bvh, writes=[bvh])
                        for (t0, n) in qtiles:
                            qn, bqn = Qn.next()
                            fw.dma("sp", qn[:, 0:n], QT[h * 192:h * 192 + 128, t0:t0 + n], bqn, writes=[bqn])
                            qp, bqp = Qp.next()
                            fw.dma("sp", qp[:, 0:n], QT[h * 192 + 128:h * 192 + 192, t0:t0 + n], bqp, writes=[bqp])
                            g, bg = gz.next()
                            fw.dma("sp", g[:, 0:n], gazT[h * 128:(h + 1) * 128, t0:t0 + n], bg, writes=[bg])
                            O, bO = Or.next()
                            L, bL = Lr.next()
                            nkb = NCORES * nblk
                            def issue_S(kb):
                                c, blk = kb // nblk, kb % nblk
                                S, bS = STr.next()
                                fw.mm([lambda pe: pe.matmul(S[:, 0:n], lhsT=knt[:, c, blk * 128:(blk + 1) * 128], rhs=qn[:, 0:n], start=True, stop=False),
                                       lambda pe: pe.matmul(S[:, 0:n], lhsT=Kpe[:, c, blk * 128:(blk + 1) * 128], rhs=qp[:, 0:n], start=False, stop=True)],
                                      reads=[bkn, BKpe, bqn, bqp], writes=[bS])
                                return S, bS

                            LOOK = 2
                            pend = [issue_S(k) for k in range(min(LOOK, nkb))]
                            for kb in range(nkb):
                                c, blk = kb // nblk, kb % nblk
                                S, bS = pend.pop(0)
                                if kb + LOOK < nkb:
                                    pend.append(issue_S(kb + LOOK))
                                p, bp = PT.next()
                                fw.op("act", lambda e: e.activation(out=p[:, 0:n], in_=S[:, 0:n], func=AF.Exp), reads=[bS], writes=[bp])
                                fw.mm([lambda pe: pe.matmul(O[:, 0:n], lhsT=vht[:, c, blk * 128:(blk + 1) * 128], rhs=p[:, 0:n], start=(kb == 0), stop=(kb == nkb - 1)),
                                       lambda pe: pe.matmul(L[:, 0:n], lhsT=ones_bf[:, :], rhs=p[:, 0:n], start=(kb == 0), stop=(kb == nkb - 1))],
                                      reads=[bvh, bp, Bconst], writes=[bO, bL])
                            rl, brl = tmp.next()
                            fw.op("dve", lambda e: e.reciprocal(out=rl[:, 0:n], in_=L[:, 0:n]), reads=[bL], writes=[brl])
                            o1, bo1 = tmp.next()
                            fw.op("dve", lambda e: e.tensor_tensor(out=o1[:, 0:n], in0=O[:, 0:n], in1=rl[:, 0:n], op=ALU.mult), reads=[bO, brl], writes=[bo1])
                            o2, bo2 = ob.next()
                            fw.op("dve", lambda e: e.tensor_tensor(out=o2[:, 0:n], in0=o1[:, 0:n], in1=g[:, 0:n], op=ALU.mult), reads=[bo1, bg], writes=[bo2])
                            fw.dma("pool", mixT[2048 + h * 128:2048 + (h + 1) * 128, t0:t0 + n], o2[:, 0:n], bo2, reads=[bo2])
                fw.barrier()


        def phase_e_edges(l):
            with ExitStack() as st:
                et = sbt(st, "et", [128, 8, 32], F32)
                Bet = fw.buf("et", dma=True)
                for si, (s0, nseg) in enumerate(((0, np_), (np_, ns))):
                    fw.dma("sp", et[:, :, si * 16:si * 16 + 8], pxT[:, s0:s0 + 8].rearrange("(j p) e -> p j e", p=128), Bet, writes=[Bet])
                    fw.dma("sp", et[:, :, si * 16 + 8:si * 16 + 16], pxT[:, s0 + nseg - 8:s0 + nseg].rearrange("(j p) e -> p j e", p=128), Bet, writes=[Bet])
                fw.dma("sp", edrec.rearrange("(j p) e -> p j e", p=128), et[:], Bet, reads=[Bet])
                fw.barrier()

        def phase_e(l):
            with ExitStack() as st:
                ea = sbt(st, "ea", [128, NCORES, 8, 32], F32)
                Bea = fw.buf("ea", dma=True)
                halo = sbt(st, "halo", [128, 8, 32], F32)
                Bhalo = fw.buf("halo")
                sel = sbt(st, "sel", [128, 16], F32)
                pcr = sbt(st, "pcr", [128, 4, 2, 16], F32)
                Bsel = fw.buf("sel", dma=True)
                pw32 = sbt(st, "pw32", [128, 8, 256], F32)
                pw = sbt(st, "pw", [128, 8, 256], BF16)
                Bpw = fw.buf("pw", dma=True)
                nmax = max(np_, ns)
                P = Ring(st, "P", 2, [128, nmax + 16], F32, dma=True)
                A = Ring(st, "A", 3, [128, nmax + 16], F32)
                df = [sbt(st, "df%d" % i, [128, nmax], BF16) for i in range(2)]
                Bdf = [fw.buf("df") for _ in range(2)]
                gz = Ring(st, "egz", 2, [128, 512], F32, dma=True)
                ob = Ring(st, "eob", 2, [128, 512], BF16, dma=True)
                pp = Ring(st, "epp", 2, [128, 512], F32, psum=True)
                for c in range(NCORES):
                    fw.dma("sp", ea[:, c, :, :], edall[c * 1024:(c + 1) * 1024, :].rearrange("(j p) e -> p j e", p=128), Bea, writes=[Bea])
                fw.dma("sp", sel[:], sel_in[:, :], Bsel, writes=[Bsel])
                fw.dma("sp", pcr[:].rearrange("p g s e -> p (g s e)"), pcorr_in[:, :], Bsel, writes=[Bsel])
                fw.dma("sp", pw32[:], pool_w[l].rearrange("g (k p) d -> p (g k) d", p=128), Bpw, writes=[Bpw])
                fw.op("dve", lambda e: e.tensor_copy(out=pw[:], in_=pw32[:]), reads=[Bpw], writes=[Bpw])
                fw.op("dve", lambda e: e.memset(halo[:], 0.0), writes=[Bhalo])
                for si in range(2):
                    for side in range(2):
                        dst = halo[:, :, si * 16 + side * 8:si * 16 + side * 8 + 8]
                        for c in range(NCORES):
                            src = ea[:, c, :, si * 16 + (8 if side == 0 else 0):si * 16 + (16 if side == 0 else 8)]
                            fw.op("dve", lambda e, dst=dst, src=src, c=c, side=side: e.scalar_tensor_tensor(
                                out=dst, in0=src, scalar=sel[:, side * 8 + c:side * 8 + c + 1], in1=dst, op0=ALU.mult, op1=ALU.add),
                                reads=[Bea, Bsel, Bhalo], writes=[Bhalo])
                for si, (s0, nseg) in enumerate(((0, np_), (np_, ns))):
                    qtiles = [(t0, n) for (t0, n) in subs if s0 <= t0 < s0 + nseg]
                    for g in range(4):
                        w = 2 << g
                        for jj in range(2):
                            j = 2 * g + jj
                            Pt, bP = P.next()
                            fw.dma("sp", Pt[:, 8:8 + nseg], pxT[j * 128:(j + 1) * 128, s0:s0 + nseg], bP, writes=[bP])
                            fw.op("dve", lambda e: e.tensor_copy(out=Pt[:, 0:8], in_=halo[:, j, si * 16:si * 16 + 8]), reads=[Bhalo], writes=[bP])
                            fw.op("dve", lambda e: e.tensor_copy(out=Pt[:, 8 + nseg:16 + nseg], in_=halo[:, j, si * 16 + 8:si * 16 + 16]), reads=[Bhalo], writes=[bP])
                            NN = nseg + 16
                            cur, bcur = A.next()
                            fw.op("dve", lambda e: e.tensor_tensor(out=cur[:, 1:NN], in0=Pt[:, 0:NN - 1], in1=Pt[:, 1:NN], op=ALU.add), reads=[bP], writes=[bcur])
                            lo, hi = 1, NN
                            sh = 1
                            for lev in range(g):
                                nxt, bnxt = A.next()
                                fw.op("dve", lambda e, cur=cur, nxt=nxt, lo=lo, hi=hi, sh=sh: e.tensor_tensor(
                                    out=nxt[:, lo + sh:hi - sh], in0=cur[:, lo:hi - 2 * sh], in1=cur[:, lo + 2 * sh:hi], op=ALU.add), reads=[bcur], writes=[bnxt])
                                cur, bcur = nxt, bnxt
                                lo, hi = lo + sh, hi - sh
                                sh *= 2
                            fw.op("dve", lambda e: e.tensor_tensor(out=cur[:, 8:16], in0=cur[:, 8:16], in1=pcr[:, g, si, 0:8], op=ALU.mult), reads=[bcur, Bsel], writes=[bcur])
                            fw.op("dve", lambda e: e.tensor_tensor(out=cur[:, nseg:nseg + 8], in0=cur[:, nseg:nseg + 8], in1=pcr[:, g, si, 8:16], op=ALU.mult),
                                  reads=[bcur, Bsel], writes=[bcur])
                            fw.op("dve", lambda e: e.scalar_tensor_tensor(out=df[jj][:, 0:nseg], in0=cur[:, 8:8 + nseg], scalar=1.0 / w, in1=Pt[:, 8:8 + nseg],
                                                                          op0=ALU.mult, op1=ALU.subtract), reads=[bcur, bP], writes=[Bdf[jj]])
                        for dd in range(2):
                            jo = 2 * g + dd
                            for (t0, n) in qtiles:
                                lo_ = t0 - s0
                                pz, bpz = gz.next()
                                fw.dma("sp", pz[:, 0:n], gpzT[jo * 128:(jo + 1) * 128, t0:t0 + n], bpz, writes=[bpz])
                                pq, bpq = pp.next()
                                fw.mm([lambda pe, kk=kk: pe.matmul(pq[:, 0:n], lhsT=pw[:, g * 2 + kk, dd * 128:(dd + 1) * 128], rhs=df[kk][:, lo_:lo_ + n],
                                                                   start=(kk == 0), stop=(kk == 1)) for kk in range(2)], reads=[Bpw, Bdf[0], Bdf[1]], writes=[bpq])
                                o, bo = ob.next()
                                fw.op("dve", lambda e: e.scalar_tensor_tensor(out=o[:, 0:n], in0=pq[:, 0:n], scalar=pc[:, l, 48 + jo:49 + jo], in1=pz[:, 0:n],
                                                                              op0=ALU.mult, op1=ALU.mult), reads=[bpq, bpz, Bconst], writes=[bo])
                                fw.dma("pool", mixT[jo * 128:(jo + 1) * 128, t0:t0 + n], o[:, 0:n], bo, reads=[bo])
                fw.barrier()


        segs = ((0, np_), (np_, ns))

        def phase_d0(l):
            with ExitStack() as st:
                g = sbt(st, "dg", [16, NT], F32); e1 = sbt(st, "de1", [16, NT], F32); ls = sbt(st, "dls", [16, NT], F32)
                on = sbt(st, "don", [16, NT], F32); Bc = sbt(st, "dB", [16, NT], F32)
                Bg = fw.buf("dg", dma=True); Be = fw.buf("de"); Bl = fw.buf("dl", dma=True); Bo = fw.buf("do"); BB = fw.buf("dB", dma=True)
                fw.dma("sp", g[:], gT[:, :], Bg, writes=[Bg])
                fw.op("dve", lambda e: e.memset(on[:], 1.0), writes=[Bo])
                fw.op("dve", lambda e: e.tensor_scalar(out=g[:], in0=g[:], scalar1=pc[0:16, l, 70:71], scalar2=None, op0=ALU.add), reads=[Bg, Bconst], writes=[Bg])
                fw.op("act", lambda e: e.activation(out=e1[:], in_=g[:], func=AF.Exp, scale=-1.0), reads=[Bg], writes=[Be])
                fw.op("act", lambda e: e.activation(out=ls[:], in_=e1[:], func=AF.Ln, bias=1.0), reads=[Be], writes=[Bl])
                fw.op("dve", lambda e: e.tensor_scalar(out=ls[:], in0=ls[:], scalar1=-1.0, scalar2=None, op0=ALU.mult), reads=[Bl], writes=[Bl])
                for (s0, nseg) in segs:
                    fw.op("dve", lambda e: e.tensor_tensor_scan(out=Bc[:, s0:s0 + nseg], data0=on[:, s0:s0 + nseg], data1=ls[:, s0:s0 + nseg], initial=0.0,
                                                                op0=ALU.mult, op1=ALU.add), reads=[Bo, Bl], writes=[BB])
                fw.dma("pool", grow[0], g[:], Bg, reads=[Bg])
                fw.dma("pool", grow[1], Bc[:], BB, reads=[BB])
                fw.dma("pool", grow[2], ls[:], Bl, reads=[Bl])
                fw.barrier()

        def phase_d1(l):
            with ExitStack() as st:
                tl = [sbt(st, "d1t%d" % i, [4, NT], F32) for i in range(8)]
                bt = [fw.buf("d1t", dma=True) for _ in range(8)]
                hb = [sbt(st, "d1h%d" % i, [4, NT], BF16) for i in range(3)]
                bh = [fw.buf("d1h", dma=True) for _ in range(3)]
                I, Bq, Lq, Aq, bq, aE, dq, rr = tl
                bI, bB, bL, bA, bb, baE, bd, brr = bt

                def split_store(x, bx, dirn, r0):
                    cur, bcur = x, bx
                    for part in range(3):
                        fw.op("dve", lambda e: e.tensor_copy(out=hb[part][:], in_=cur[:]), reads=[bcur], writes=[bh[part]])
                        fw.dma("pool", drow[dirn * 4:dirn * 4 + 4, r0 + part, :], hb[part][:], bh[part], reads=[bh[part]])
                        if part < 2:
                            fw.op("dve", lambda e: e.tensor_tensor(out=rr[:], in0=cur[:], in1=hb[part][:], op=ALU.subtract), reads=[bcur, bh[part], brr], writes=[brr])
                            cur, bcur = rr, brr

                fw.dma("sp", I[:], grow[0, 0:4, :], bI, writes=[bI])
                fw.dma("sp", Bq[:], grow[1, 4:8, :], bB, writes=[bB])
                fw.op("dve", lambda e: e.tensor_tensor(out=Aq[:], in0=I[:], in1=Bq[:], op=ALU.subtract), reads=[bI, bB], writes=[bA])
                for (s0, nseg) in segs:
                    fw.op("dve", lambda e: e.tensor_scalar(out=aE[:, s0:s0 + nseg], in0=Aq[:, s0:s0 + nseg], scalar1=Bq[:, s0 + nseg - 1:s0 + nseg], scalar2=None, op0=ALU.add),
                          reads=[bA, bB], writes=[baE])
                split_store(Aq, bA, 0, 0)
                split_store(Bq, bB, 0, 3)
                split_store(aE, baE, 0, 6)
                split_store(Bq, bB, 0, 9)
                fw.dma("sp", I[:], grow[0, 8:12, :], bI, writes=[bI])
                fw.dma("sp", Bq[:], grow[1, 12:16, :], bB, writes=[bB])
                fw.dma("sp", Lq[:], grow[2, 12:16, :], bL, writes=[bL])
                fw.op("dve", lambda e: e.tensor_tensor(out=Lq[:], in0=Bq[:], in1=Lq[:], op=ALU.subtract), reads=[bB, bL], writes=[bL])
                fw.op("dve", lambda e: e.tensor_tensor(out=Aq[:], in0=I[:], in1=Lq[:], op=ALU.add), reads=[bI, bL], writes=[bA])
                fw.op("dve", lambda e: e.tensor_scalar(out=bq[:], in0=Lq[:], scalar1=-1.0, scalar2=None, op0=ALU.mult), reads=[bL], writes=[bb])
                for (s0, nseg) in segs:
                    fw.op("dve", lambda e: e.tensor_scalar(out=dq[:, s0:s0 + nseg], in0=bq[:, s0:s0 + nseg], scalar1=Bq[:, s0 + nseg - 1:s0 + nseg], scalar2=None, op0=ALU.add),
                          reads=[bb, bB], writes=[bd])
                split_store(Aq, bA, 1, 0)
                split_store(bq, bb, 1, 3)
                split_store(Aq, bA, 1, 6)
                split_store(dq, bd, 1, 9)
                fw.barrier()

        def phase_d2(l):
            with ExitStack() as st:
                nmax = max(np_, ns)
                kt = sbt(st, "d2k", [128, nmax // 128, 256], BF16); Bk = fw.buf("d2k", dma=True)
                vt = sbt(st, "d2v", [128, nmax // 128, 257], BF16); Bv = fw.buf("d2v", dma=True)
                R3 = Ring(st, "d2r", 2, [3, nmax], BF16, dma=True)
                T3 = Ring(st, "d2t", 2, [3, nmax], BF16, dma=True)
                acol = Ring(st, "d2a", 3, [128, 1], F32)
                ka = Ring(st, "d2ka", 3, [128, 256], BF16)
                Et = Ring(st, "d2E", 2, [128, 2, 258], F32, dma=True)
                pcl = Ring(st, "d2pc", 2, [128, 8], F32, psum=True)
                pE = Ring(st, "d2pE", 4, [128, 512], F32, psum=True)
                for si, (s0, nseg) in enumerate(segs):
                    nblk = nseg // 128
                    for h in range(4):
                        fw.dma("sp", kt[:, 0:nblk, :], ktok[s0:s0 + nseg, h * 256:(h + 1) * 256].rearrange("(b p) d -> p b d", p=128), Bk, writes=[Bk])
                        fw.op("dve", lambda e: e.memset(vt[:], 1.0), writes=[Bv])
                        fw.dma("sp", vt[:, 0:nblk, 0:256], vtok[s0:s0 + nseg, h * 256:(h + 1) * 256].rearrange("(b p) d -> p b d", p=128), Bv, writes=[Bv])
                        for dirn in range(2):
                            ch = si * 8 + dirn * 4 + h
                            r3, br3 = R3.next()
                            fw.dma("sp", r3[:, 0:nseg], drow[dirn * 4 + h, 6:9, s0:s0 + nseg], br3, writes=[br3])
                            t3, bt3 = T3.next()
                            rT = 3 if dirn == 0 else 9
                            fw.dma("sp", t3[:, 0:nseg], drow[dirn * 4 + h, rT:rT + 3, s0:s0 + nseg], bt3, writes=[bt3])
                            E0, bE0 = pE.next()
                            E1, bE1 = pE.next()
                            for b in range(nblk):
                                pc1, bpc1 = pcl.next()
                                fw.mm([lambda pe: pe.matmul(pc1[:, 0:1], lhsT=r3[0:3, b * 128:(b + 1) * 128], rhs=ones_bf[0:3, 0:1], start=True, stop=True)],
                                      reads=[br3, Bconst], writes=[bpc1])
                                a, ba = acol.next()
                                fw.op("act", lambda e: e.activation(out=a[:, 0:1], in_=pc1[:, 0:1], func=AF.Exp), reads=[bpc1], writes=[ba])
                                k2, bk2 = ka.next()
                                fw.op("dve", lambda e: e.tensor_scalar(out=k2[:, :], in0=kt[:, b, :], scalar1=a[:, 0:1], scalar2=None, op0=ALU.mult), reads=[Bk, ba], writes=[bk2])
                                fw.mm([lambda pe: pe.matmul(E0[:, 0:257], lhsT=k2[:, 0:128], rhs=vt[:, b, :], start=(b == 0), stop=(b == nblk - 1)),
                                       lambda pe: pe.matmul(E1[:, 0:257], lhsT=k2[:, 128:256], rhs=vt[:, b, :], start=(b == 0), stop=(b == nblk - 1))],
                                      reads=[bk2, Bv], writes=[bE0, bE1])
                            et, bet = Et.next()
                            fw.op("dve", lambda e: e.tensor_copy(out=et[:, 0, 0:257], in_=E0[:, 0:257]), reads=[bE0], writes=[bet])
                            fw.op("act", lambda e: e.copy(out=et[:, 1, 0:257], in_=E1[:, 0:257]), reads=[bE1], writes=[bet])
                            tcol = (nseg - 1) if dirn == 0 else 0
                            pc1, bpc1 = pcl.next()
                            fw.mm([lambda pe: pe.matmul(pc1[:, 0:1], lhsT=ones_bf[0:3, :], rhs=t3[0:3, tcol:tcol + 1], start=True, stop=True)], reads=[bt3, Bconst], writes=[bpc1])
                            fw.op("dve", lambda e: e.tensor_copy(out=et[:, 0, 257:258], in_=pc1[:, 0:1]), reads=[bpc1], writes=[bet])
                            fw.op("dve", lambda e: e.tensor_copy(out=et[:, 1, 257:258], in_=pc1[:, 0:1]), reads=[bpc1], writes=[bet])
                            fw.dma("pool", Erec[ch * 256:(ch + 1) * 256, :].rearrange("(k p) e -> p k e", p=128), et[:], bet, reads=[bet])
                fw.barrier()

        def phase_d34(l):
            with ExitStack() as st:
                nmax = max(np_, ns)
                Cin = sbt(st, "Cin", [128, 16, 2, 256], BF16)
                nrep = sbt(st, "nrep", [128, 16, 2, 128], BF16)
                BC = fw.buf("Cin")
                onesf = sbt(st, "onesf", [128, 128], F32)
                selm = sbt(st, "selm", [128, 16], F32)
                msk = sbt(st, "msk", [128, 2, 4, 512], BF16)
                Bsm = fw.buf("selm", dma=True)
                fw.dma("sp", selm[:], selm_in[:, :], Bsm, writes=[Bsm])
                fw.dma("sp", msk[:].rearrange("p a r t -> p (a r t)"), mask_in[:, :], Bsm, writes=[Bsm])
                fw.op("dve", lambda e: e.memset(onesf[:], 1.0), writes=[Bsm])
                with ExitStack() as s3:
                    Ea = Ring(s3, "Ea", 2, [128, NCORES, 2, 258], F32, dma=True)
                    S = sbt(s3, "S", [128, 2, 257], F32); BS = fw.buf("S")
                    tm = sbt(s3, "tm", [128, 2, 257], F32); Btm = fw.buf("tm")
                    sc = Ring(s3, "sc", 4, [128, 2], F32)
                    for ch in range(16):
                        dirn = (ch // 4) % 2
                        ea, bea = Ea.next()
                        for c in range(NCORES):
                            fw.dma("sp", ea[:, c, :, :], Eall[c * 4096 + ch * 256:c * 4096 + (ch + 1) * 256, :].rearrange("(k p) e -> p k e", p=128), bea, writes=[bea])
                        fw.op("dve", lambda e: e.memset(S[:], 0.0), writes=[BS])
                        order = range(NCORES) if dirn == 0 else range(NCORES - 1, -1, -1)
                        for c in order:
                            mcol = selm[:, dirn * 8 + c:dirn * 8 + c + 1]
                            d1, bd1 = sc.next()
                            fw.op("act", lambda e: e.activation(out=d1[:, 0:1], in_=ea[:, c, 0, 257:258], func=AF.Exp), reads=[bea], writes=[bd1])
                            fw.op("dve", lambda e: e.tensor_scalar(out=d1[:, 1:2], in0=d1[:, 0:1], scalar1=-1.0, scalar2=mcol, op0=ALU.add, op1=ALU.mult), reads=[bd1, Bsm], writes=[bd1])
                            fw.op("dve", lambda e: e.tensor_scalar(out=d1[:, 1:2], in0=d1[:, 1:2], scalar1=1.0, scalar2=None, op0=ALU.add), reads=[bd1], writes=[bd1])
                            fw.op("dve", lambda e: e.tensor_scalar(out=tm[:], in0=ea[:, c, :, 0:257], scalar1=mcol, scalar2=None, op0=ALU.mult), reads=[bea, Bsm], writes=[Btm])
                            fw.op("dve", lambda e: e.scalar_tensor_tensor(out=S[:], in0=S[:], scalar=d1[:, 1:2], in1=tm[:], op0=ALU.mult, op1=ALU.add), reads=[BS, bd1, Btm], writes=[BS])
                        fw.op("dve", lambda e: e.tensor_copy(out=Cin[:, ch, :, :], in_=S[:, :, 0:256]), reads=[BS], writes=[BC])
                        for dc in range(2):
                            fw.op("dve", lambda e: e.tensor_scalar(out=nrep[:, ch, dc, :], in0=onesf[:, :], scalar1=S[:, dc, 256:257], scalar2=None, op0=ALU.mult),
                                  reads=[BS, Bsm], writes=[BC])
                    fw.barrier()
                with ExitStack() as s4:
                    qh = sbt(s4, "qh", [128, 2, nmax], BF16); Bqh = fw.buf("qh", dma=True)
                    kh = sbt(s4, "kh", [128, 2, nmax], BF16); Bkh = fw.buf("kh", dma=True)
                    vh = sbt(s4, "vh", [128, nmax // 128, 256], BF16); Bvh = fw.buf("vh", dma=True)
                    L6 = Ring(s4, "L6", 2, [6, nmax], BF16, dma=True)
                    R6 = Ring(s4, "R6", 2, [6, nmax], BF16, dma=True)
                    D3r = Ring(s4, "D3r", 2, [3, nmax], BF16, dma=True)
                    qd = Ring(s4, "qd", 2, [128, 2, 512], BF16)
                    f32 = Ring(s4, "m32", 6, [128, 512], F32, dma=True)
                    W = Ring(s4, "W", 3, [128, 512], BF16)
                    pA = Ring(s4, "pA", 2, [128, 512], F32, psum=True)
                    pD = Ring(s4, "pD", 2, [128, 512], F32, psum=True)
                    pO = [pst(s4, "pO%d" % i, [128, 512], F32) for i in range(3)]
                    BpO = fw.buf("pO")
                    for si, (s0, nseg) in enumerate(segs):
                        nblk = nseg // 128
                        qtiles = [(t0, n) for (t0, n) in subs if s0 <= t0 < s0 + nseg]
                        for h in range(4):
                            fw.dma("sp", qh[:, :, 0:nseg], qT[h * 256:(h + 1) * 256, s0:s0 + nseg].rearrange("(k p) t -> p k t", p=128), Bqh, writes=[Bqh])
                            fw.dma("sp", kh[:, :, 0:nseg], kT[h * 256:(h + 1) * 256, s0:s0 + nseg].rearrange("(k p) t -> p k t", p=128), Bkh, writes=[Bkh])
                            fw.dma("sp", vh[:, 0:nblk, :], vtok[s0:s0 + nseg, h * 256:(h + 1) * 256].rearrange("(b p) d -> p b d", p=128), Bvh, writes=[Bvh])
                            for dirn in range(2):
                                ch = si * 8 + dirn * 4 + h
                                l6, bl6 = L6.next(); r6, br6 = R6.next(); d3, bd3 = D3r.next()
                                fw.op("dve", lambda e: e.memset(l6[:], 1.0), writes=[bl6])
                                fw.op("dve", lambda e: e.memset(r6[:], 1.0), writes=[br6])
                                fw.dma("sp", l6[0:3, 0:nseg], drow[dirn * 4 + h, 0:3, s0:s0 + nseg], bl6, writes=[bl6])
                                fw.dma("sp", r6[3:6, 0:nseg], drow[dirn * 4 + h, 3:6, s0:s0 + nseg], br6, writes=[br6])
                                fw.dma("sp", d3[:, 0:nseg], drow[dirn * 4 + h, 9:12, s0:s0 + nseg], bd3, writes=[bd3])
                                for (t0, n) in qtiles:
                                    lo_ = t0 - s0
                                    qb0, nqb = lo_ // 128, n // 128
                                    pd, bpd = pD.next()
                                    fw.mm([lambda pe: pe.matmul(pd[:, 0:n], lhsT=ones_bf[0:3, :], rhs=d3[0:3, lo_:lo_ + n], start=True, stop=True)], reads=[bd3, Bconst], writes=[bpd])
                                    ed, bed = f32.next()
                                    fw.op("act", lambda e: e.activation(out=ed[:, 0:n], in_=pd[:, 0:n], func=AF.Exp), reads=[bpd], writes=[bed])
                                    q2, bq2 = qd.next()
                                    for dc in range(2):
                                        fw.op("dve", lambda e, dc=dc: e.tensor_tensor(out=q2[:, dc, 0:n], in0=qh[:, dc, lo_:lo_ + n], in1=ed[:, 0:n], op=ALU.mult), reads=[Bqh, bed], writes=[bq2])
                                    kbs = list(range(0, qb0 + nqb)) if dirn == 0 else list(range(qb0, nblk))
                                    fw.mm([lambda pe, ec=ec, dc=dc: pe.matmul(pO[ec][:, 0:n], lhsT=Cin[:, ch, dc, ec * 128:(ec + 1) * 128], rhs=q2[:, dc, 0:n], start=(dc == 0), stop=False)
                                           for ec in range(2) for dc in range(2)] +
                                          [lambda pe, dc=dc: pe.matmul(pO[2][:, 0:n], lhsT=nrep[:, ch, dc, :], rhs=q2[:, dc, 0:n], start=(dc == 0), stop=False) for dc in range(2)],
                                          reads=[BC, bq2], writes=[BpO])
                                    def issue_AD(kb):
                                        a_, ba_ = pA.next()
                                        fw.mm([lambda pe, dc=dc: pe.matmul(a_[:, 0:n], lhsT=kh[:, dc, kb * 128:(kb + 1) * 128], rhs=qh[:, dc, lo_:lo_ + n], start=(dc == 0), stop=(dc == 1))
                                               for dc in range(2)], reads=[Bkh, Bqh], writes=[ba_])
                                        d_, bd_ = pD.next()
                                        fw.mm([lambda pe: pe.matmul(d_[:, 0:n], lhsT=l6[0:6, kb * 128:(kb + 1) * 128], rhs=r6[0:6, lo_:lo_ + n], start=True, stop=True)],
                                              reads=[bl6, br6], writes=[bd_])
                                        return a_, ba_, d_, bd_

                                    pend = [issue_AD(kbs[0])]
                                    for ki, kb in enumerate(kbs):
                                        last = (ki == len(kbs) - 1)
                                        a_, ba_, d_, bd_ = pend.pop(0)
                                        if not last:
                                            pend.append(issue_AD(kbs[ki + 1]))
                                        diag = qb0 <= kb < qb0 + nqb
                                        e_, be_ = f32.next()
                                        if diag:
                                            c_, bc_ = f32.next()
                                            fw.op("dve", lambda e: e.tensor_scalar(out=c_[:, 0:n], in0=d_[:, 0:n], scalar1=40.0, scalar2=None, op0=ALU.min), reads=[bd_], writes=[bc_])
                                            fw.op("act", lambda e: e.activation(out=e_[:, 0:n], in_=c_[:, 0:n], func=AF.Exp), reads=[bc_], writes=[be_])
                                        else:
                                            fw.op("act", lambda e: e.activation(out=e_[:, 0:n], in_=d_[:, 0:n], func=AF.Exp), reads=[bd_], writes=[be_])
                                        w_, bw_ = W.next()
                                        if diag:
                                            fw.op("dve", lambda e: e.tensor_tensor(out=e_[:, 0:n], in0=e_[:, 0:n], in1=msk[:, dirn, kb - qb0, 0:n], op=ALU.mult), reads=[be_, Bsm], writes=[be_])
                                        fw.op("dve", lambda e: e.tensor_tensor(out=w_[:, 0:n], in0=a_[:, 0:n], in1=e_[:, 0:n], op=ALU.mult), reads=[ba_, be_], writes=[bw_])
                                        fw.mm([lambda pe, ec=ec: pe.matmul(pO[ec][:, 0:n], lhsT=vh[:, kb, ec * 128:(ec + 1) * 128], rhs=w_[:, 0:n], start=False, stop=last) for ec in range(2)] +
                                              [lambda pe: pe.matmul(pO[2][:, 0:n], lhsT=ones_bf[:, :], rhs=w_[:, 0:n], start=False, stop=last)],
                                              reads=[Bvh, bw_, Bconst], writes=[BpO])
                                    dn, bdn = f32.next()
                                    fw.op("act", lambda e: e.activation(out=dn[:, 0:n], in_=pO[2][:, 0:n], func=AF.Abs), reads=[BpO], writes=[bdn])
                                    fw.op("dve", lambda e: e.tensor_scalar(out=dn[:, 0:n], in0=dn[:, 0:n], scalar1=1.0, scalar2=None, op0=ALU.max), reads=[bdn], writes=[bdn])
                                    fw.op("dve", lambda e: e.reciprocal(out=dn[:, 0:n], in_=dn[:, 0:n]), reads=[bdn], writes=[bdn])
                                    for ec in range(2):
                                        o_, bo_ = f32.next()
                                        fw.op("dve", lambda e, ec=ec: e.tensor_tensor(out=o_[:, 0:n], in0=pO[ec][:, 0:n], in1=dn[:, 0:n], op=ALU.mult), reads=[BpO, bdn], writes=[bo_])
                                        fw.dma("pool", hbuf[dirn, h * 256 + ec * 128:h * 256 + (ec + 1) * 128, t0:t0 + n], o_[:, 0:n], bo_, reads=[bo_])
                    fw.barrier()

        def phase_d5(l):
            with ExitStack() as st:
                hf = Ring(st, "hf", 2, [128, 2, 512], F32, dma=True)
                hb_ = Ring(st, "hb", 2, [128, 2, 512], F32, dma=True)
                go = Ring(st, "go", 2, [128, 2, 512], F32, dma=True)
                gm = Ring(st, "gm", 2, [128, 2, 512], F32, dma=True)
                sq = Ring(st, "d5sq", 2, [128, 2, 512], BF16)
                f32 = Ring(st, "d5f", 4, [128, 512], F32)
                ob = Ring(st, "d5o", 3, [128, 512], BF16, dma=True)
                pp = Ring(st, "d5p", 2, [128, 512], F32, psum=True)
                for (t0, n) in subs:
                    for h in range(4):
                        a, ba = hf.next(); b_, bb = hb_.next(); g1, bg1 = go.next(); g2, bg2 = gm.next()
                        rows = slice(1024 * 0 + h * 256, h * 256 + 256)
                        fw.dma("sp", a[:, :, 0:n], hbuf[0, h * 256:(h + 1) * 256, t0:t0 + n].rearrange("(k p) t -> p k t", p=128), ba, writes=[ba])
                        fw.dma("sp", b_[:, :, 0:n], hbuf[1, h * 256:(h + 1) * 256, t0:t0 + n].rearrange("(k p) t -> p k t", p=128), bb, writes=[bb])
                        fw.dma("sp", g1[:, :, 0:n], goT[h * 256:(h + 1) * 256, t0:t0 + n].rearrange("(k p) t -> p k t", p=128), bg1, writes=[bg1])
                        fw.dma("sp", g2[:, :, 0:n], gmzT[h * 256:(h + 1) * 256, t0:t0 + n].rearrange("(k p) t -> p k t", p=128), bg2, writes=[bg2])
                        fw.op("dve", lambda e: e.tensor_tensor(out=a[:, :, 0:n], in0=a[:, :, 0:n], in1=b_[:, :, 0:n], op=ALU.add), reads=[ba, bb], writes=[ba])
                        s_, bs_ = sq.next()
                        fw.op("dve", lambda e: e.tensor_tensor(out=s_[:, :, 0:n], in0=a[:, :, 0:n], in1=a[:, :, 0:n], op=ALU.mult), reads=[ba], writes=[bs_])
                        p_, bp_ = pp.next()
                        fw.mm([lambda pe, ec=ec: pe.matmul(p_[:, 0:n], lhsT=ones_bf[:, :], rhs=s_[:, ec, 0:n], start=(ec == 0), stop=(ec == 1)) for ec in range(2)],
                              reads=[bs_, Bconst], writes=[bp_])
                        r_, br_ = rstd_from(f32, p_[:, 0:n], 128, n, 1.0 / 256, [bp_])
                        fw.op("dve", lambda e: e.tensor_tensor(out=g1[:, :, 0:n], in0=g1[:, :, 0:n], in1=g2[:, :, 0:n], op=ALU.mult), reads=[bg1, bg2], writes=[bg1])
                        for ec in range(2):
                            t_, bt_ = f32.next()
                            fw.op("dve", lambda e, ec=ec: e.scalar_tensor_tensor(out=t_[:, 0:n], in0=a[:, ec, 0:n], scalar=pc[:, l, 56 + h * 2 + ec:57 + h * 2 + ec], in1=r_[:, 0:n],
                                                                               op0=ALU.mult, op1=ALU.mult), reads=[ba, br_, Bconst], writes=[bt_])
                            o_, bo_ = ob.next()
                            fw.op("dve", lambda e, ec=ec: e.tensor_tensor(out=o_[:, 0:n], in0=t_[:, 0:n], in1=g1[:, ec, 0:n], op=ALU.mult), reads=[bt_, bg1], writes=[bo_])
                            fw.dma("pool", mixT[1024 + h * 256 + ec * 128:1024 + h * 256 + (ec + 1) * 128, t0:t0 + n], o_[:, 0:n], bo_, reads=[bo_])
                fw.barrier()

        def zero_mix(r0, r1):
            with ExitStack() as st:
                z = sbt(st, "zt", [128, NT], BF16)
                Bz = fw.buf("z", dma=True)
                fw.op("dve", lambda e: e.memset(z[:], 0.0), writes=[Bz])
                for r in range(r0, r1, 128):
                    fw.dma("pool", mixT[r:r + 128, :], z[:], Bz, reads=[Bz])
                fw.barrier()

        for l in range(DEPTH):
            prep_weights(l)
        for l in range(DEPTH):
            xsrc = x_in if l == 0 else x1
            ydst = x1 if l == 0 else y_out
            phase_a(l, xsrc)
            phase_b(l)
            phase_e_edges(l)
            phase_d0(l)
            phase_d1(l)
            phase_d2(l)
            ag = lambda src, dst: (lambda e: e.collective_compute("AllGather", ALU.bypass, replica_groups=[list(range(NCORES))],
                                                                  ins=[src.opt()], outs=[dst.opt()]))
            fw.collective(cc_ed, ag(edrec, edall))
            fw.collective(cc_E, ag(Erec, Eall))
            fw.collective(cc_kv, ag(kvrec, kvall))
            fw.wait_key(cc_ed)
            phase_e(l)
            fw.wait_key(cc_E)
            phase_d34(l)
            phase_d5(l)
            fw.wait_key(cc_kv)
            phase_c(l)
            phase_f(l, xsrc, ydst)
        fw.barrier()
        print("instructions:", fw.ninst)
    return nc


def _mk_win_blocks():
    blocks = []
    specs = []

    def addF(src0, ncols, dname, post, scale=1.0):
        for j in range(ncols // 512):
            blocks.append(([(src0 + j * 512, 512, 0)], scale))
            specs.append(("F", [(q * 128, 128, dname, j * 512 + q * 128, post) for q in range(4)]))

    def addT(src0, ncols, dname, scale=1.0):
        for j in range(ncols // 512):
            blocks.append(([(src0 + j * 512, 512, 0)], scale))
            specs.append(("T", dname, j * 512))

    addF(0, 1024, "pxT", "copy")
    addF(1024, 1024, "gpzT", "silu")
    addF(2048, 1024, "qT", "copy")
    addF(3072, 1024, "kT", "copy", 1.0 / 16)
    addT(3072, 1024, "ktok", 1.0 / 16)
    addT(4096, 1024, "vtok")
    addF(5120, 1024, "goT", "sigmoid")
    addF(6144, 1024, "gmzT", "silu")
    addF(7184, 1536, "qlatT", "copy")
    addF(8720, 512, "kvlatT", "copy")
    addF(9296, 2048, "gazT", "silu")
    blocks.append(([(7168, 16, 0), (9232, 64, 128), (9264, 32, 256), (9232, 32, 288)], 1.0))
    specs.append(("S", [(0, 16, "gT", 0, "copy"), (128, 64, "krT", 0, "copy"), (256, 64, "krsT", 0, "copy")]))
    return blocks, specs


WIN_BLOCKS, A_SPECS = _mk_win_blocks()


def _mk_dmask():
    sidx = np.arange(128)[:, None, None]
    r = np.arange(4)[None, :, None]
    t = np.arange(512)[None, None, :]
    fwd = (t >= r * 128 + sidx)
    bwd = (t <= r * 128 + sidx)
    mk = np.stack([fwd, bwd], 1).astype(np.float32)
    return np.ascontiguousarray(mk.reshape(128, 2 * 4 * 512)).astype(ml_dtypes.bfloat16)


DMASK = _mk_dmask()
assert len(WIN_BLOCKS) == 25


def _pcols(norm_g, qlat_g, kvlat_g, pool_scale, mlstm_norm_g, qn_g, qr_g, kn_g, kr_g, gate_bias):
    pc = np.zeros((DEPTH, 128, NPC), np.float32)
    for l in range(DEPTH):
        pc[l, :, 0:32] = norm_g[l].reshape(32, 128).T
        pc[l, :, 32:44] = qlat_g[l].reshape(12, 128).T
        pc[l, :, 44:48] = kvlat_g[l].reshape(4, 128).T
        pc[l, :, 48:56] = pool_scale[l].reshape(8, 128).T
        pc[l, :, 56:64] = mlstm_norm_g[l].reshape(8, 128).T
        pc[l, :, 64] = qn_g[l]
        pc[l, :, 65] = kn_g[l]
        pc[l, 0:64, 66] = qr_g[l]
        pc[l, 0:64, 67] = np.concatenate([qr_g[l][32:], qr_g[l][:32]])
        pc[l, 0:64, 68] = kr_g[l]
        pc[l, 0:64, 69] = np.concatenate([kr_g[l][32:], kr_g[l][:32]])
        pc[l, 0:16, 70] = gate_bias[l]
    return pc


def _rope_tables(pos):
    inv_freq = (10000.0 ** (-(np.arange(0, 64, 2, dtype=np.float32) / 64.0))).astype(np.float32)
    ang = pos.astype(np.float32)[None, :] * inv_freq[:, None]
    c = np.cos(ang).astype(np.float32)
    s = np.sin(ang).astype(np.float32)
    return np.concatenate([c, c], 0), np.concatenate([-s, s], 0)


def run(inputs, sp, ss, dbg=()):
    np_, ns = sp // NCORES, ss // NCORES
    nc = build(np_, ns, dbg)
    f = lambda a: np.ascontiguousarray(np.asarray(a, dtype=np.float32))
    xp = f(inputs["x_prompt"])[0]
    xs = f(inputs["x_sample"])[0]
    pc = _pcols(*[f(inputs[k]) for k in ("norm_g", "qlat_g", "kvlat_g", "pool_scale", "mlstm_norm_g", "qn_g", "qr_g", "kn_g", "kr_g", "gate_bias")])
    ident = np.eye(128, dtype=np.float32).astype(ml_dtypes.bfloat16)
    shared = {"w_in": f(inputs["w_in"]), "w_out": f(inputs["w_out"]), "w_uq": f(inputs["w_uq"]), "w_ukv": f(inputs["w_ukv"]),
              "pool_w": f(inputs["pool_w"]), "pcols": pc, "ident": ident}
    in_maps = []
    for c in range(NCORES):
        pos = np.concatenate([np.arange(c * np_, (c + 1) * np_), np.arange(c * ns, (c + 1) * ns)])
        ct, stb = _rope_tables(pos)
        m = dict(shared)
        m["x"] = np.ascontiguousarray(np.concatenate([xp[c * np_:(c + 1) * np_], xs[c * ns:(c + 1) * ns]], 0))
        sel = np.zeros((128, 16), np.float32)
        if c > 0:
            sel[:, c - 1] = 1.0
        if c < NCORES - 1:
            sel[:, 8 + c + 1] = 1.0
        pcorr = np.ones((128, 4, 2, 16), np.float32)
        for g in range(4):
            w = 2 << g
            for si, (nloc, S) in enumerate(((np_, sp), (ns, ss))):
                for e in range(8):
                    t = c * nloc + e
                    cnt = min(t + w // 2, S) - max(t - w // 2, 0)
                    pcorr[:, g, si, e] = w / cnt
                    t = c * nloc + nloc - 8 + e
                    cnt = min(t + w // 2, S) - max(t - w // 2, 0)
                    pcorr[:, g, si, 8 + e] = w / cnt
        selm = np.zeros((128, 16), np.float32)
        selm[:, 0:c] = 1.0
        selm[:, 8 + c + 1:16] = 1.0
        m["selm"] = selm
        m["dmask"] = DMASK
        m["sel"] = sel
        m["pcorr"] = np.ascontiguousarray(pcorr.reshape(128, 128))
        m["cosT"] = np.ascontiguousarray(ct)
        m["sinT"] = np.ascontiguousarray(stb)
        in_maps.append(m)
    res = run_bass_kernel_spmd(nc, in_maps, core_ids=list(range(NCORES)))
    if dbg:
        return res.results
    ys = [np.asarray(r["y"]) for r in res.results]
    yp = np.concatenate([y[:np_] for y in ys], 0)[None]
    ysm = np.concatenate([y[np_:] for y in ys], 0)[None]
    return yp.astype(np.float32), ysm.astype(np.float32)


def kernel(**inputs):
    sp = inputs["x_prompt"].shape[1]
    ss = inputs["x_sample"].shape[1]
    return run(inputs, sp, ss)
```
